# Optimizing a Trainium2 kernel written in Bass

```python
import math
import jax, jax.numpy as jnp
from jax import lax
import numpy as np

D_MODEL = 1024
BATCH = 2
SEQ = 16384
DEPTH = 2

CONV_CH = D_MODEL // 4
CONV_GROUPS = 4
CONV_WIDTH = 31
RET_VDIM = 64
RET_KDIM = RET_VDIM // 2
RET_HEADS = (3 * D_MODEL // 8) // RET_VDIM
RET_CHUNK = 128
ATT_HDIM = 64
ATT_HEADS = (3 * D_MODEL // 8) // ATT_HDIM
DILATED_BRANCHES = ((128, 1), (512, 4), (2048, 16))
ATT_BLOCK = 128
N_BUCKETS = 32
MAX_DISTANCE = 2048
D_FF = 4 * D_MODEL
EPS = 1e-6
ROPE_BASE = 10000.0
NEG_INF = -1e30

RET_W = RET_HEADS * RET_VDIM
ATT_W = ATT_HEADS * ATT_HDIM
MIX_W = CONV_CH + RET_W + ATT_W
IN_SIZES = (2 * CONV_CH, RET_HEADS * RET_KDIM, RET_HEADS * RET_KDIM, RET_W, RET_W, ATT_W, ATT_W, ATT_W)
IN_W = sum(IN_SIZES)
IN_SPLITS = [sum(IN_SIZES[:i + 1]) for i in range(len(IN_SIZES) - 1)]

kernel_name = "hybrid_conv_retention_dilated_attn"

F32 = jnp.float32


def rms_norm(x, g):
    xf = x.astype(F32)
    return xf * lax.rsqrt(jnp.mean(xf * xf, axis=-1, keepdims=True) + EPS) * g.astype(F32)


def conv_mixer(u, conv_w, conv_b, conv_g):
    B, S, _ = u.shape
    a, gate = jnp.split(u, 2, axis=-1)
    h = a * jax.nn.sigmoid(gate)
    h = lax.conv_general_dilated(
        h, conv_w.astype(F32)[:, None, :], window_strides=(1,),
        padding=[(CONV_WIDTH - 1, 0)],
        dimension_numbers=("NWC", "WIO", "NWC"),
        feature_group_count=CONV_CH) + conv_b.astype(F32)
    hg = h.reshape(B, S, CONV_GROUPS, CONV_CH // CONV_GROUPS)
    hg = hg * lax.rsqrt(jnp.mean(hg * hg, axis=-1, keepdims=True) + EPS)
    h = hg.reshape(B, S, CONV_CH) * conv_g.astype(F32)
    return jax.nn.silu(h)


def rotary(x, pos):
    d = x.shape[-1]
    inv = 1.0 / (ROPE_BASE ** jnp.linspace(0.0, 1.0, d // 2, dtype=F32))
    ang = pos[:, None] * inv[None, :]
    c, s = jnp.cos(ang)[:, None, :], jnp.sin(ang)[:, None, :]
    x1, x2 = x[..., 0::2], x[..., 1::2]
    return jnp.stack([x1 * c - x2 * s, x1 * s + x2 * c], axis=-1).reshape(x.shape)


def retention(q, k, v, g, ret_g):
    B, S, H, dk = q.shape
    dv = v.shape[-1]
    C = RET_CHUNK
    N = S // C
    pos = jnp.arange(S, dtype=F32)
    q = rotary(q, pos)
    k = rotary(k, pos) * (dk ** -0.5)
    to_chunks = lambda t: t.reshape(B, N, C, H, t.shape[-1]).transpose(0, 3, 1, 2, 4)
    qc, kc, vc = to_chunks(q), to_chunks(k), to_chunks(v)
    log_gamma = jnp.log(1.0 - 2.0 ** (-5.0 - jnp.arange(H, dtype=F32)))
    idx = jnp.arange(C, dtype=F32)
    diff = idx[:, None] - idx[None, :]
    decay = jnp.where(diff >= 0, jnp.exp(log_gamma[:, None, None] * jnp.maximum(diff, 0.0)), 0.0)
    scores = jnp.einsum("bhnid,bhnjd->bhnij", qc, kc) * decay[None, :, None]
    inner = jnp.einsum("bhnij,bhnje->bhnie", scores, vc)
    k_dec = kc * jnp.exp(log_gamma[:, None] * (C - 1.0 - idx))[None, :, None, :, None]
    kv = jnp.einsum("bhnjd,bhnje->bhnde", k_dec, vc)
    chunk_decay = jnp.exp(log_gamma * C)[None, :, None, None]

    def step(state, kv_n):
        return state * chunk_decay + kv_n, state

    _, states = lax.scan(step, jnp.zeros((B, H, dk, dv), F32), jnp.moveaxis(kv, 2, 0))
    states = jnp.moveaxis(states, 0, 2)
    q_dec = qc * jnp.exp(log_gamma[:, None] * (idx + 1.0))[None, :, None, :, None]
    cross = jnp.einsum("bhnid,bhnde->bhnie", q_dec, states)
    o = (inner + cross).transpose(0, 2, 3, 1, 4).reshape(B, S, H, dv)
    o = o * lax.rsqrt(jnp.mean(o * o, axis=-1, keepdims=True) + EPS)
    o = o.reshape(B, S, H * dv) * ret_g.astype(F32)
    return o * jax.nn.silu(g)


def t5_bucket(dist):
    max_exact = N_BUCKETS // 2
    nf = jnp.maximum(dist, 1).astype(F32)
    large = max_exact + (jnp.log(nf / max_exact) / math.log(MAX_DISTANCE / max_exact)
                         * (N_BUCKETS - max_exact)).astype(jnp.int32)
    large = jnp.minimum(large, N_BUCKETS - 1)
    return jnp.where(dist < max_exact, dist, large)


def dilated_branch(q, k, v, rel_bias, dilation):
    B, H, S, hd = q.shape
    Lb = ATT_BLOCK
    span = Lb * dilation
    S_pad = -(-S // span) * span
    L = S_pad // dilation
    nb = L // Lb

    def gather(t):
        t = jnp.pad(t, ((0, 0), (0, 0), (0, S_pad - S), (0, 0)))
        return t.reshape(B, H, L, dilation, hd).transpose(0, 1, 3, 2, 4).reshape(B, H, dilation, nb, Lb, hd)

    def with_prev(t):
        prev = jnp.concatenate([jnp.zeros_like(t[:, :, :, :1]), t[:, :, :, :-1]], axis=3)
        return jnp.concatenate([prev, t], axis=4)

    qg = gather(q)
    kk, vv = with_prev(gather(k)), with_prev(gather(v))
    qi = jnp.arange(Lb, dtype=jnp.int32)[:, None]
    kj = jnp.arange(2 * Lb, dtype=jnp.int32)[None, :]
    dist = Lb + qi - kj
    band = (dist >= 0) & (dist <= Lb)
    valid = band[None] & ((jnp.arange(nb)[:, None, None] > 0) | (kj[None] >= Lb))
    bias = rel_bias.astype(F32)[t5_bucket(jnp.clip(dist, 0, Lb) * dilation)]
    bias = bias.transpose(2, 0, 1)[None, :, None, None]
    logits = jnp.einsum("bhgnqd,bhgnkd->bhgnqk", qg, kk) * (hd ** -0.5) + bias
    logits = jnp.where(valid, logits, NEG_INF)
    m = jnp.max(logits, axis=-1, keepdims=True)
    p = jnp.exp(logits - m)
    s = jnp.sum(p, axis=-1, keepdims=True)
    o = jnp.einsum("bhgnqk,bhgnkd->bhgnqd", p, vv) / s
    lse = (m + jnp.log(s))[..., 0]
    o = o.reshape(B, H, dilation, L, hd).transpose(0, 1, 3, 2, 4).reshape(B, H, S_pad, hd)[:, :, :S]
    lse = lse.reshape(B, H, dilation, L).transpose(0, 1, 3, 2).reshape(B, H, S_pad)[:, :, :S]
    return o, lse


def dilated_attention(q, k, v, q_g, k_g, rel_bias):
    B, S, H, hd = q.shape
    q = rms_norm(q, q_g).transpose(0, 2, 1, 3)
    k = rms_norm(k, k_g).transpose(0, 2, 1, 3)
    v = v.transpose(0, 2, 1, 3)
    outs, lses = [], []
    for _, dilation in DILATED_BRANCHES:
        o, lse = dilated_branch(q, k, v, rel_bias, dilation)
        outs.append(o)
        lses.append(lse)
    w = jax.nn.softmax(jnp.stack(lses, axis=0), axis=0)
    o = jnp.sum(w[..., None] * jnp.stack(outs, axis=0), axis=0)
    return o.transpose(0, 2, 1, 3).reshape(B, S, H * hd)


def hybrid_layer(x, norm1_g, w_in, conv_w, conv_b, conv_g, ret_g, q_g, k_g,
                 w_out, norm2_g, w_ff1, w_ff2, rel_bias):
    B, S, _ = x.shape
    h = rms_norm(x, norm1_g)
    u = h @ w_in.astype(F32)
    u_conv, rq, rk, rv, rg, aq, ak, av = jnp.split(u, IN_SPLITS, axis=-1)
    conv_out = conv_mixer(u_conv, conv_w, conv_b, conv_g)
    ret_out = retention(rq.reshape(B, S, RET_HEADS, RET_KDIM), rk.reshape(B, S, RET_HEADS, RET_KDIM),
                        rv.reshape(B, S, RET_HEADS, RET_VDIM), rg, ret_g)
    att_out = dilated_attention(aq.reshape(B, S, ATT_HEADS, ATT_HDIM), ak.reshape(B, S, ATT_HEADS, ATT_HDIM),
                                av.reshape(B, S, ATT_HEADS, ATT_HDIM), q_g, k_g, rel_bias)
    mix = jnp.concatenate([conv_out, ret_out, att_out], axis=-1) @ w_out.astype(F32)
    x = x + mix.astype(x.dtype)
    h2 = rms_norm(x, norm2_g)
    ff = jnp.square(jax.nn.relu(h2 @ w_ff1.astype(F32))) @ w_ff2.astype(F32)
    return x + ff.astype(x.dtype)


def setup_inputs(seed: int = 0) -> dict:
    key = jax.random.key(seed)
    ks = jax.random.split(key, 15)
    nrm = lambda k, shape: jax.random.normal(k, shape, F32)
    return {
        "x": nrm(ks[0], (BATCH, SEQ, D_MODEL)),
        "norm1_g": 1.0 + 0.02 * nrm(ks[1], (DEPTH, D_MODEL)),
        "w_in": nrm(ks[2], (DEPTH, D_MODEL, IN_W)) * D_MODEL ** -0.5,
        "conv_w": nrm(ks[3], (DEPTH, CONV_WIDTH, CONV_CH)) * CONV_WIDTH ** -0.5,
        "conv_b": 0.02 * nrm(ks[4], (DEPTH, CONV_CH)),
        "conv_g": 1.0 + 0.02 * nrm(ks[5], (DEPTH, CONV_CH)),
        "ret_g": 1.0 + 0.02 * nrm(ks[6], (DEPTH, RET_W)),
        "q_g": 1.0 + 0.02 * nrm(ks[7], (DEPTH, ATT_HDIM)),
        "k_g": 1.0 + 0.02 * nrm(ks[8], (DEPTH, ATT_HDIM)),
        "w_out": nrm(ks[9], (DEPTH, MIX_W, D_MODEL)) * MIX_W ** -0.5,
        "norm2_g": 1.0 + 0.02 * nrm(ks[10], (DEPTH, D_MODEL)),
        "w_ff1": nrm(ks[11], (DEPTH, D_MODEL, D_FF)) * D_MODEL ** -0.5,
        "w_ff2": nrm(ks[12], (DEPTH, D_FF, D_MODEL)) * D_FF ** -0.5,
        "rel_bias": 0.1 * nrm(ks[13], (N_BUCKETS, ATT_HEADS)),
    }


def reference(x, norm1_g, w_in, conv_w, conv_b, conv_g, ret_g, q_g, k_g,
              w_out, norm2_g, w_ff1, w_ff2, rel_bias):
    for l in range(DEPTH):
        x = hybrid_layer(x, norm1_g[l], w_in[l], conv_w[l], conv_b[l], conv_g[l], ret_g[l],
                         q_g[l], k_g[l], w_out[l], norm2_g[l], w_ff1[l], w_ff2[l], rel_bias)
    return x
```

```python
import math
from contextlib import ExitStack

import numpy as np
import concourse.bass as bass
import concourse.mybir as mybir
from concourse.bass_utils import run_bass_kernel_spmd

F32 = mybir.dt.float32
BF16 = mybir.dt.bfloat16
AF = mybir.ActivationFunctionType
ALU = mybir.AluOpType
AX = mybir.AxisListType

D = 1024
DFF = 4096
INW = 2816
EPS = 1e-6
NCORE = 8
NQ = 4
SEQ = 16384
NT_FULL = SEQ // NQ // 128
HALO = 16
RING = 18
PD = 2304
XJ = 2176
C_CONV, C_RQ, C_RK, C_RV, C_RG, C_AQ, C_AK, C_AV = 0, 512, 704, 896, 1280, 1664, 2048, 2432


class Prog:
    ENG = ('pe', 'act', 'dve', 'pool', 'sp')

    def __init__(self):
        self.ops = {e: [] for e in self.ENG}
        self.cnt = {e: 0 for e in self.ENG}
        self.dcnt = {}
        self.state = {}
        self.selfsync = {'pe': False, 'act': True, 'dve': True, 'pool': True, 'sp': False}

    def _deps(self, reads, writes):
        deps = {}

        def add(s, v):
            if deps.get(s, 0) < v:
                deps[s] = v
        for k in reads:
            st = self.state.get(k)
            if st and st[0]:
                add(*st[0])
        for k in writes:
            st = self.state.get(k)
            if st:
                if st[0]:
                    add(*st[0])
                for s, v in st[1].items():
                    add(s, v)
        return deps

    def _commit(self, sv, reads, writes):
        for k in reads:
            st = self.state.setdefault(k, [None, {}])
            if st[1].get(sv[0], 0) < sv[1]:
                st[1][sv[0]] = sv[1]
        for k in writes:
            self.state[k] = [sv, {}]

    def op(self, eng, method, reads=(), writes=(), **kw):
        fn = (lambda m, k: (lambda e: getattr(e, m)(**k)))(method, kw)
        deps = self._deps(reads, writes)
        self.cnt[eng] += 1
        sv = (eng, self.cnt[eng])
        if not self.selfsync[eng]:
            deps.pop(eng, None)
        self.ops[eng].append(('op', fn, deps, None))
        self._commit(sv, reads, writes)

    def dma(self, queue, sem, reads=(), writes=(), **kw):
        fn = (lambda k: (lambda e: e.dma_start(**k)))(kw)
        deps = self._deps(reads, writes)
        key = 'dma:' + sem
        self.dcnt[key] = self.dcnt.get(key, 0) + 16
        sv = (key, self.dcnt[key])
        self.ops[queue].append(('dma', fn, deps, key))
        self._commit(sv, reads, writes)

    def seal(self, sem, keys):
        key = 'dma:' + sem
        for k in keys:
            self.state[k] = [(key, self.dcnt[key]), {}]

    def final_wait_all(self, queue='sp'):
        deps = dict(self.dcnt)
        for e in ('pe', 'act', 'dve', 'pool'):
            if self.cnt[e]:
                deps[e] = self.cnt[e]
        self.ops[queue].append(('wait', None, deps, None))

    def emit(self, nc):
        with ExitStack() as es:
            semh = {}
            for e in ('pe', 'act', 'dve', 'pool'):
                semh[e] = es.enter_context(nc.semaphore('s_' + e))
            for k in self.dcnt:
                semh[k] = es.enter_context(nc.semaphore('s_' + k.replace(':', '_')))
            block = es.enter_context(nc.Block())
            names = {'pe': 'tensor', 'act': 'scalar', 'dve': 'vector', 'pool': 'gpsimd', 'sp': 'sync'}

            def mk(ename):
                ops = self.ops[ename]

                def body(eng):
                    waited = {}
                    for kind, fn, deps, key in ops:
                        for s, v in deps.items():
                            if waited.get(s, 0) >= v:
                                continue
                            eng.wait_ge(semh[s], v)
                            waited[s] = v
                        if kind == 'wait':
                            continue
                        ins = fn(eng)
                        if kind == 'op':
                            ins.then_inc(semh[ename], 1)
                        else:
                            ins.then_inc(semh[key], 16)
                return body
            for ename in self.ENG:
                if self.ops[ename]:
                    getattr(block, names[ename])(mk(ename))


class Ctx:
    def __init__(self):
        self.nc = bass.Bass("TRN2", target_bir_lowering=False)
        self.P = Prog()
        self.es = ExitStack()

    def sb(self, name, shape, dt):
        return self.es.enter_context(self.nc.sbuf_tensor(name, shape, dt))

    def ps(self, name, shape, dt):
        return self.es.enter_context(self.nc.psum_tensor(name, shape, dt))

    def din(self, name, shape, dt=F32):
        return self.nc.dram_tensor(name, list(shape), dt, kind="ExternalInput").ap()

    def dout(self, name, shape, dt=F32):
        return self.nc.dram_tensor(name, list(shape), dt, kind="ExternalOutput").ap()

    def finish(self):
        self.P.final_wait_all('sp')
        self.P.emit(self.nc)
        self.es.close()
        return self.nc


def bcast_rows(ap2d, n):
    return bass.AP(ap2d.tensor, ap2d.offset, [[0, 128], [1, n]])


def emit_norm_T(P, xt, xkey, Gb, xn, junk, ss, rstd, tp, hT_dst, hTkey, ident):
    P.op('act', 'activation', [xkey], ['junk', 'ss'], out=junk[:], in_=xt, func=AF.Square, accum_out=ss[:])
    P.op('act', 'activation', ['ss'], ['rstd'], out=rstd[:], in_=ss[:], func=AF.Ln, scale=1.0 / D, bias=EPS)
    P.op('act', 'activation', ['rstd'], ['rstd'], out=rstd[:], in_=rstd[:], func=AF.Exp, scale=-0.5)
    P.op('dve', 'scalar_tensor_tensor', [xkey, 'rstd', 'Gb'], ['xn'], out=xn[:], in0=xt, scalar=rstd[:, 0:1],
         in1=Gb[:], op0=ALU.mult, op1=ALU.mult)
    for kc in range(8):
        P.op('pe', 'transpose', ['xn', 'ident'], ['B0'], out=tp[:, kc * 128:(kc + 1) * 128],
             in_=xn[:, kc * 128:(kc + 1) * 128], identity=ident[:])
    P.op('act', 'activation', ['B0'], [hTkey], out=hT_dst, in_=tp[:].rearrange("p (k t) -> p k t", k=8), func=AF.Copy)


def emit_sigmoid(P, src, srckeys, dst, dstkey, tmp, tmpkey):
    P.op('act', 'activation', srckeys, [tmpkey], out=tmp, in_=src, func=AF.Exp, scale=-1.0)
    P.op('act', 'activation', [tmpkey], [tmpkey], out=tmp, in_=tmp, func=AF.Ln, bias=1.0)
    P.op('act', 'activation', [tmpkey], [dstkey], out=dst, in_=tmp, func=AF.Exp, scale=-1.0)


def build_stage_b(NT):
    C = Ctx()
    nc, P = C.nc, C.P
    T = NT * 128
    x1 = C.din("x1", [T, D])
    g2 = C.din("g2", [1, D])
    w1 = C.din("w1", [D, DFF])
    w2 = C.din("w2", [DFF, D])
    idn = C.din("idn", [128, 128])
    x2 = C.dout("x2", [T, D])
    W1 = C.sb("W1", [128, 8, DFF], BF16)
    W2 = C.sb("W2", [128, 32, D], BF16)
    Gb = C.sb("Gb", [128, D], F32)
    ident = C.sb("ident", [128, 128], BF16)
    NX = 4
    xs = [C.sb(f"xs{i}", [128, D], F32) for i in range(NX)]
    xn = C.sb("xn", [128, D], BF16)
    junk = C.sb("junk", [128, D], BF16)
    ss = C.sb("ss", [128, 1], F32)
    rstd = C.sb("rstd", [128, 1], F32)
    hT = [C.sb(f"hT{i}", [128, 8, 256], BF16) for i in range(2)]
    hid = C.sb("hid", [128, 32, 256], BF16)
    rl = [C.sb(f"rl{i}", [128, 512], F32) for i in range(2)]
    ys = [C.sb(f"ys{i}", [128, D], F32) for i in range(2)]
    tp = C.ps("tp", [128, D], BF16)
    ph = [C.ps(f"ph{i}", [128, 512], F32) for i in range(2)]
    po = [C.ps(f"po{i}", [128, 512], F32) for i in range(4)]

    P.dma('pool', 'ident', [], ['ident'], out=ident[:], in_=idn[:, :])
    P.dma('sp', 'Gb', [], ['Gb'], out=Gb[:], in_=bcast_rows(g2, D))
    w1v = w1.rearrange("(kc p) n -> p kc n", p=128)
    w2v = w2.rearrange("(kc p) n -> p kc n", p=128)
    for kc in range(8):
        P.dma('pool', 'W1', [], [f'W1:{kc}'], out=W1[:, kc, :], in_=w1v[:, kc, :])
    for kc in range(32):
        P.dma('pool', 'W2', [], [f'W2:{kc}'], out=W2[:, kc, :], in_=w2v[:, kc, :])
    P.seal('W1', [f'W1:{kc}' for kc in range(8)])
    P.seal('W2', [f'W2:{kc}' for kc in range(32)])

    def load_x(t):
        s = t % NX
        P.dma('sp', f'xs{s}', [], [f'xs{s}'], out=xs[s][:], in_=x1[t * 128:(t + 1) * 128, :])

    def prep(g):
        for i in range(2):
            t = 2 * g + i
            s = t % NX
            emit_norm_T(P, xs[s][:], f'xs{s}', Gb, xn, junk, ss, rstd, tp,
                        hT[g % 2][:, :, i * 128:(i + 1) * 128], f'hT{g % 2}:{i}', ident)

    NG = NT // 2
    for t in range(min(NX, NT)):
        load_x(t)
    prep(0)
    for g in range(NG):
        h = hT[g % 2]
        hk = [f'hT{g % 2}:0', f'hT{g % 2}:1']
        for fp in range(16):
            pb = ph[fp % 2]
            for half in range(2):
                fc = fp * 2 + half
                for kc in range(8):
                    P.op('pe', 'matmul', [f'W1:{kc}'] + hk, [f'ph{fp % 2}'],
                         out=pb[:, half * 256:(half + 1) * 256], lhsT=W1[:, kc, fc * 128:(fc + 1) * 128],
                         rhs=h[:, kc, :], start=(kc == 0), stop=(kc == 7))
            r = rl[fp % 2]
            P.op('act', 'activation', [f'ph{fp % 2}'], [f'rl{fp % 2}'], out=r[:], in_=pb[:], func=AF.Relu)
            P.op('dve', 'tensor_tensor', [f'ph{fp % 2}', f'rl{fp % 2}'], [f'hid:{fp}'],
                 out=hid[:, 2 * fp:2 * fp + 2, :], in0=r[:].rearrange("p (a t) -> p a t", a=2),
                 in1=pb[:].rearrange("p (a t) -> p a t", a=2), op=ALU.mult)
        if g + 1 < NG:
            prep(g + 1)
        for i in range(2):
            t = 2 * g + i
            s = t % NX
            for dh in range(2):
                pi = i * 2 + dh
                pb = po[pi]
                for fc in range(32):
                    P.op('pe', 'matmul', [f'hid:{fc // 2}', f'W2:{fc}'], [f'po{pi}'],
                         out=pb[:], lhsT=hid[:, fc, i * 128:(i + 1) * 128], rhs=W2[:, fc, dh * 512:(dh + 1) * 512],
                         start=(fc == 0), stop=(fc == 31))
                P.op('dve', 'tensor_tensor', [f'po{pi}', f'xs{s}'], [f'ys{i}'],
                     out=ys[i][:, dh * 512:(dh + 1) * 512], in0=pb[:], in1=xs[s][:, dh * 512:(dh + 1) * 512], op=ALU.add)
            P.dma('sp', f'ys{i}', [f'ys{i}'], [f'x2:{t}'], out=x2[t * 128:(t + 1) * 128, :], in_=ys[i][:])
            if t + NX < NT:
                load_x(t + NX)
    return C.finish()


def emit_rotary(P, pt, ptkey, cs, cskey, out, outkey, tmps, n):
    xe, xo = pt[:, 0:2 * n:2], pt[:, 1:2 * n:2]
    c, s = cs[:, 0, 0:n], cs[:, 1, 0:n]
    t1, t2 = tmps[0][:, 0:n], tmps[1][:, 0:n]
    P.op('dve', 'tensor_tensor', [ptkey, cskey], ['rt1'], out=t1, in0=xe, in1=c, op=ALU.mult)
    P.op('dve', 'tensor_tensor', [ptkey, cskey], ['rt2'], out=t2, in0=xo, in1=s, op=ALU.mult)
    P.op('dve', 'tensor_tensor', ['rt1', 'rt2'], [outkey], out=out[:, 0:2 * n:2], in0=t1, in1=t2, op=ALU.subtract)
    P.op('dve', 'tensor_tensor', [ptkey, cskey], ['rt1'], out=t1, in0=xo, in1=c, op=ALU.mult)
    P.op('dve', 'tensor_tensor', [ptkey, cskey], ['rt2'], out=t2, in0=xe, in1=s, op=ALU.mult)
    P.op('dve', 'tensor_tensor', ['rt1', 'rt2'], [outkey], out=out[:, 1:2 * n:2], in0=t1, in1=t2, op=ALU.add)


def build_stage_p(NT):
    C = Ctx()
    nc, P = C.nc, C.P
    T = NT * 128
    xin = C.din("xin", [T, D])
    g1 = C.din("g1", [1, D])
    win = C.din("win", [D, INW])
    idn = C.din("idn", [128, 128])
    csk = C.din("csk", [NT, 128, 2, 96])
    dend = C.din("dend", [NT, 128, 192])
    aout = C.dout("aout", [2, 96, 384])
    Wk = C.sb("Wk", [128, 8, 576], BF16)
    Gb = C.sb("Gb", [128, D], F32)
    ident = C.sb("ident", [128, 128], BF16)
    xs = [C.sb(f"xs{i}", [128, D], F32) for i in range(2)]
    xn = C.sb("xn", [128, D], BF16)
    junk = C.sb("junk", [128, D], BF16)
    ss = C.sb("ss", [128, 1], F32)
    rstd = C.sb("rstd", [128, 1], F32)
    hT = C.sb("hT", [128, 8, 128], BF16)
    cs = [C.sb(f"cs{i}", [128, 2, 96], F32) for i in range(2)]
    de = [C.sb(f"de{i}", [128, 192], F32) for i in range(2)]
    tmps = [C.sb(f"rtmp{i}", [128, 192], F32) for i in range(2)]
    krot = C.sb("krot", [128, 192], F32)
    kdec = C.sb("kdec", [128, 192], BF16)
    vb = C.sb("vb", [128, 384], BF16)
    ao = C.sb("ao", [96, 2, 384], F32)
    tp = C.ps("tp", [128, D], BF16)
    pk = C.ps("pk", [128, 512], F32)
    pv = C.ps("pv", [128, 512], F32)
    pa = [C.ps(f"pa{i}", [128, 512], F32) for i in range(2)]

    P.dma('pool', 'ident', [], ['ident'], out=ident[:], in_=idn[:, :])
    P.dma('sp', 'Gb', [], ['Gb'], out=Gb[:], in_=bcast_rows(g1, D))
    wv = win.rearrange("(kc p) n -> p kc n", p=128)
    for kc in range(8):
        P.dma('pool', 'Wk', [], ['Wk'], out=Wk[:, kc, 0:192], in_=wv[:, kc, C_RK:C_RK + 192])
        P.dma('pool', 'Wk', [], ['Wk'], out=Wk[:, kc, 192:576], in_=wv[:, kc, C_RV:C_RV + 384])
    P.seal('Wk', ['Wk'])

    def load(t):
        s = t % 2
        P.dma('sp', f'xs{s}', [], [f'xs{s}'], out=xs[s][:], in_=xin[t * 128:(t + 1) * 128, :])
        P.dma('sp', f'cs{s}', [], [f'cs{s}'], out=cs[s][:], in_=csk[t])
        P.dma('sp', f'de{s}', [], [f'de{s}'], out=de[s][:], in_=dend[t])

    load(0)
    for t in range(NT):
        s = t % 2
        if t + 1 < NT:
            load(t + 1)
        emit_norm_T(P, xs[s][:], f'xs{s}', Gb, xn, junk, ss, rstd, tp, hT[:], 'hT', ident)
        for kc in range(8):
            P.op('pe', 'matmul', ['Wk', 'hT'], ['pk'], out=pk[:, 0:192], lhsT=hT[:, kc, :], rhs=Wk[:, kc, 0:192],
                 start=(kc == 0), stop=(kc == 7))
        for kc in range(8):
            P.op('pe', 'matmul', ['Wk', 'hT'], ['pv'], out=pv[:, 0:384], lhsT=hT[:, kc, :], rhs=Wk[:, kc, 192:576],
                 start=(kc == 0), stop=(kc == 7))
        emit_rotary(P, pk, 'pk', cs[s], f'cs{s}', krot, 'krot', tmps, 96)
        P.op('dve', 'tensor_tensor', ['krot', f'de{s}'], ['kdec'], out=kdec[:], in0=krot[:], in1=de[s][:], op=ALU.mult)
        P.op('act', 'activation', ['pv'], ['vb'], out=vb[:], in_=pv[:, 0:384], func=AF.Copy)
        for blk in range(2):
            P.op('pe', 'matmul', ['kdec', 'vb'], [f'pa{blk}'], out=pa[blk][0:96, 0:384],
                 lhsT=kdec[:, blk * 96:(blk + 1) * 96], rhs=vb[:], start=(t == 0), stop=(t == NT - 1))
    for blk in range(2):
        P.op('act', 'activation', [f'pa{blk}'], ['ao'], out=ao[:, blk, :], in_=pa[blk][0:96, 0:384], func=AF.Copy)
    P.dma('sp', 'ao', ['ao'], ['aout'], out=aout.rearrange("b p n -> p b n"), in_=ao[:])
    return C.finish()


def build_stage_a(NT, dbg_tiles=None, dbg_phase=9):
    C = Ctx()
    nc, P = C.nc, C.P
    TT = (HALO + NT) * 128
    xin = C.din("xin", [TT, D])
    g1 = C.din("g1", [1, D])
    win = C.din("win", [D, INW])
    wout = C.din("wout", [D, D])
    cwT = C.din("cwT", [256, 31])
    cbg = C.din("cbg", [128, 4])
    retg = C.din("retg", [1, 384])
    qg = C.din("qg", [1, 64])
    kg = C.din("kg", [1, 64])
    relb = C.din("relb", [32, 6])
    idn = C.din("idn", [128, 128])
    bdg = C.din("bdg", [128, 128])
    csq = C.din("csq", [NT, 128, 2, 192])
    dect = C.din("dect", [128, 6, 128])
    qdt = C.din("qdt", [96, 2, 128])
    kdt = C.din("kdt", [128, 192])
    g128 = C.din("g128", [96, 2, 64])
    c2 = C.din("c2", [32, PD])
    flag = C.din("flag", [128, HALO])
    sin0 = C.din("sin0", [96, 8, 2, 64])
    scoef = C.din("scoef", [96, 8, 2, 64])
    rscr = nc.dram_tensor("rscr", [6, 128 * PD], BF16, kind="Internal").ap()
    x1 = C.dout("x1", [NT * 128, D])

    Win = C.sb("Win", [128, 8, INW], BF16)
    Wout = C.sb("Wout", [128, 8, D], BF16)
    Tab = C.sb("Tab", [128, 6, XJ], BF16)
    Dg = C.sb("Dg", [128, 2, 31, 128], BF16)
    KT = C.sb("KT", [128, 3, RING * 128], BF16)
    VA = C.sb("VA", [128, RING, 6, 65], BF16)
    Gb = C.sb("Gb", [128, D], F32)
    Rgb = C.sb("Rgb", [128, 384], F32)
    Qgb = C.sb("Qgb", [128, 384], F32)
    Kgb = C.sb("Kgb", [128, 384], F32)
    ident = C.sb("ident", [128, 128], BF16)
    identf = C.sb("identf", [128, 128], F32)
    Bd = C.sb("Bd", [128, 128], BF16)
    cw = C.sb("cw", [128, 2, 31], F32)
    cb = C.sb("cb", [128, 4], F32)
    decT = C.sb("decT", [128, 6, 128], F32)
    qdT = C.sb("qdT", [128, 2, 128], BF16)
    kdT = C.sb("kdT", [128, 192], F32)
    G128 = C.sb("G128", [96, 2, 64], F32)
    flg = C.sb("flg", [128, HALO], F32)
    xs = [C.sb(f"xs{i}", [128, D], F32) for i in range(2)]
    cs = [C.sb(f"cs{i}", [128, 2, 192], F32) for i in range(2)]
    xn = C.sb("xn", [128, D], BF16)
    junk = C.sb("junk", [128, D], BF16)
    ss = C.sb("ss", [128, 1], F32)
    rstd = C.sb("rstd", [128, 1], F32)
    hT = C.sb("hT", [128, 8, 128], BF16)
    hgl = C.sb("hgl", [128, 2, 160], BF16)
    sg = C.sb("sg", [128, 2, 128], F32)
    cf = C.sb("cf", [128, 2, 128], F32)
    sqb = C.sb("sqb", [128, 2, 128], BF16)
    rs = C.sb("rs", [128, 2, 128], F32)
    cn = C.sb("cn", [128, 2, 128], F32)
    mixT = C.sb("mixT", [128, 8, 128], BF16)
    tmps = [C.sb(f"rtmp{i}", [128, 192], F32) for i in range(2)]
    qkr = C.sb("qkr", [128, 384], F32)
    qkb = C.sb("qkb", [128, 384], BF16)
    kdec = C.sb("kdec", [128, 192], BF16)
    qT = C.sb("qT", [128, 2, 128], BF16)
    kTr = C.sb("kTr", [128, 2, 128], BF16)
    qdecT = C.sb("qdecT", [128, 2, 128], BF16)
    rvb = C.sb("rvb", [128, 384], BF16)
    PTr = C.sb("PTr", [128, 6, 128], BF16)
    Sst = C.sb("Sst", [96, 2, 64], F32)
    Sbf = C.sb("Sbf", [96, 2, 64], BF16)
    f384 = [C.sb(f"f384_{i}", [128, 384], F32) for i in range(4)]
    b384 = C.sb("b384", [128, 384], BF16)
    s6 = C.sb("s6", [128, 6], F32)
    r6 = C.sb("r6", [128, 6], F32)
    QT = C.sb("QT", [128, 3, 128], BF16)
    eb = [C.sb(f"eb{i}", [128, 512], BF16) for i in range(2)]
    PT = [C.sb(f"PT{i}", [128, 512], BF16) for i in range(2)]
    ys = C.sb("ys", [128, D], F32)
    Esb = C.sb("Esb", [32, 6], F32)
    B = [C.ps(f"B{i}", [128, 512], F32) for i in range(1, 8)]
    B0 = C.ps("B0", [128, D], BF16)
    tp = B0
    B1, B2, B3, B4, B5, B6, B7 = B

    P.dma('pool', 'c_ident', [], ['ident'], out=ident[:], in_=idn[:, :])
    P.dma('sp', 'c_identf', [], ['identf'], out=identf[:], in_=idn[:, :])
    P.dma('pool', 'c_bd', [], ['Bd'], out=Bd[:], in_=bdg[:, :])
    P.dma('sp', 'c_Gb', [], ['Gb'], out=Gb[:], in_=bcast_rows(g1, D))
    P.dma('sp', 'c_Rgb', [], ['Rgb'], out=Rgb[:], in_=bcast_rows(retg, 384))
    P.dma('sp', 'c_Qgb', [], ['Qgb'], out=Qgb[:].rearrange("p (h d) -> p h d", h=6),
          in_=bass.AP(qg.tensor, 0, [[0, 128], [0, 6], [1, 64]]))
    P.dma('sp', 'c_Kgb', [], ['Kgb'], out=Kgb[:].rearrange("p (h d) -> p h d", h=6),
          in_=bass.AP(kg.tensor, 0, [[0, 128], [0, 6], [1, 64]]))
    P.dma('sp', 'c_cw', [], ['cw'], out=cw[:], in_=cwT.rearrange("(c p) j -> p c j", p=128))
    P.dma('sp', 'c_cb', [], ['cb'], out=cb[:], in_=cbg[:, :])
    P.dma('sp', 'c_decT', [], ['decT'], out=decT[:], in_=dect[:, :, :])
    P.op('pool', 'memset', [], ['qdT'], ap=qdT[:].rearrange("p b t -> p (b t)"), constant=0.0)
    P.dma('pool', 'c_qdT', [], ['qdT'], out=qdT[0:96, :, :], in_=qdt[:, :, :])
    P.dma('sp', 'c_kdT', [], ['kdT'], out=kdT[:], in_=kdt[:, :])
    P.dma('sp', 'c_g128', [], ['G128'], out=G128[:], in_=g128[:, :, :])
    P.dma('sp', 'c_flag', [], ['flg'], out=flg[:], in_=flag[:, :])
    P.dma('sp', 'c_E', [], ['Esb'], out=Esb[:], in_=relb[:, :])
    wv = win.rearrange("(kc p) n -> p kc n", p=128)
    wov = wout.rearrange("(kc p) n -> p kc n", p=128)
    for kc in range(8):
        P.dma('pool', 'Win', [], ['Win'], out=Win[:, kc, :], in_=wv[:, kc, :])
    for kc in range(8):
        P.dma('pool', 'Wout', [], ['Wout'], out=Wout[:, kc, :], in_=wov[:, kc, :])
    P.seal('Win', ['Win'])
    P.seal('Wout', ['Wout'])

    P.op('act', 'activation', ['Esb'], ['Esb'], out=Esb[:], in_=Esb[:], func=AF.Exp)
    nch = (PD + 511) // 512
    ktkeys = [f'KT{i}' for i in range(RING)]
    Fsb = KT[0:6, 0, :]
    for ci in range(nch):
        lo = ci * 512
        n = min(512, PD - lo)
        P.dma('sp', 'c_C2', [], ['ys'], out=ys[0:32, 0:n], in_=c2[:, lo:lo + n])
        P.op('pe', 'matmul', ['Esb', 'ys'], ['B1'], out=B1[0:6, 0:n], lhsT=Esb[:], rhs=ys[0:32, 0:n],
             start=True, stop=True)
        P.op('act', 'activation', ['B1'], ktkeys, out=KT[0:6, 0, lo:lo + n], in_=B1[0:6, 0:n], func=AF.Copy)
    P.dma('sp', 'c_rscr', ktkeys, ['rscr'], out=rscr.rearrange("h (r n) -> h r n", r=128),
          in_=bass.AP(Fsb.tensor, Fsb.offset, [[3 * RING * 128, 6], [0, 128], [1, PD]]))
    for h in range(6):
        P.dma('sp', 'c_Tab', ['rscr'], ['Tab'], out=Tab[:, h, :],
              in_=bass.AP(rscr.tensor, h * 128 * PD + 127, [[PD - 1, 128], [1, XJ]]))
    P.seal('c_Tab', ['Tab'])
    for c in range(2):
        for j in range(31):
            P.op('pool', 'tensor_scalar', ['identf', 'cw'], ['Dg'], out=Dg[:, c, j, :], in0=identf[:],
                 scalar1=cw[:, c, j:j + 1], scalar2=None, op0=ALU.mult)
    P.op('pool', 'memset', [], ['VA'], ap=VA[:].rearrange("p r h e -> p (r h e)"), constant=1.0)
    big = xs[0][0:96, :].rearrange("p (r x) -> p r x", r=8)
    big2 = xs[1][0:96, :].rearrange("p (r x) -> p r x", r=8)
    P.dma('sp', 'c_s0', [], ['xs0'], out=big, in_=sin0.rearrange("p r b e -> p r (b e)"))
    P.dma('sp', 'c_s1', [], ['xs1'], out=big2, in_=scoef.rearrange("p r b e -> p r (b e)"))
    P.op('dve', 'tensor_tensor', ['xs0', 'xs1'], ['xs0'], out=big, in0=big, in1=big2, op=ALU.mult)
    P.op('dve', 'tensor_reduce', ['xs0'], ['Sst'], out=Sst[:].rearrange("p b e -> p (b e)"),
         in_=big.rearrange("p r x -> p x r"), axis=AX.X, op=ALU.add)
    P.op('dve', 'tensor_copy', ['Sst'], ['Sbf'], out=Sbf[:], in_=Sst[:])

    def load(ti):
        s = ti % 2
        P.dma('sp', f'xs{s}', [], [f'xs{s}'], out=xs[s][:], in_=xin[ti * 128:(ti + 1) * 128, :])
        t = ti - HALO
        if t >= 0:
            P.dma('sp', f'cs{s}', [], [f'cs{s}'], out=cs[s][:], in_=csq[t])

    def proj_tok(col, n, bank, bkey):
        for kc in range(8):
            P.op('pe', 'matmul', ['Win', 'hT'], [bkey], out=bank[:, 0:n], lhsT=hT[:, kc, :],
                 rhs=Win[:, kc, col:col + n], start=(kc == 0), stop=(kc == 7))

    def head_rstd(src, srckeys, n_over):
        P.op('act', 'activation', srckeys, ['sq384'], out=f384[3][:], in_=src, func=AF.Square)
        P.op('dve', 'tensor_reduce', ['sq384'], ['s6'], out=s6[:], in_=f384[3][:].rearrange("p (h d) -> p h d", h=6),
             axis=AX.X, op=ALU.add)
        P.op('act', 'activation', ['s6'], ['r6'], out=r6[:], in_=s6[:], func=AF.Ln, scale=1.0 / n_over, bias=EPS)
        P.op('act', 'activation', ['r6'], ['r6'], out=r6[:], in_=r6[:], func=AF.Exp, scale=-0.5)

    def r6b():
        return r6[:].unsqueeze(2).to_broadcast([128, 6, 64])

    def v3(ap):
        return ap.rearrange("p (h d) -> p h d", h=6)

    load(0)
    for ti in range(HALO + NT):
        t = ti - HALO
        s = ti % 2
        slot = ti % RING
        if ti + 1 < HALO + NT:
            load(ti + 1)
        own = t >= 0
        if dbg_tiles is not None and ti >= dbg_tiles:
            continue
        emit_norm_T(P, xs[s][:], f'xs{s}', Gb, xn, junk, ss, rstd, tp, hT[:], 'hT', ident)

        if dbg_phase < 1:
            continue
        if own or t == -1:
            for cc in range(4):
                for kc in range(8):
                    P.op('pe', 'matmul', ['Win', 'hT'], ['B1'], out=B1[:, cc * 128:(cc + 1) * 128],
                         lhsT=Win[:, kc, cc * 128:(cc + 1) * 128], rhs=hT[:, kc, :], start=(kc == 0), stop=(kc == 7))
            if own:
                P.op('pool', 'tensor_copy', ['hgl'], ['hgl'], out=hgl[:, :, 0:30], in_=hgl[:, :, 128:158])
            emit_sigmoid(P, B1[:, 256:512].rearrange("p (c t) -> p c t", c=2), ['B1'], sg[:], 'sg', sg[:], 'sg')
            P.op('dve', 'tensor_tensor', ['B1', 'sg'], ['hgl'], out=hgl[:, :, 30:158],
                 in0=B1[:, 0:256].rearrange("p (c t) -> p c t", c=2), in1=sg[:], op=ALU.mult)
        if own:
            for c in range(2):
                for j in range(31):
                    P.op('pe', 'matmul', ['Dg', 'hgl'], ['B2'], out=B2[:, c * 128:(c + 1) * 128],
                         lhsT=Dg[:, c, j, :], rhs=hgl[:, c, j:j + 128], start=(j == 0), stop=(j == 30))
            for c in range(2):
                P.op('act', 'activation', ['B2', 'cb'], ['cf'], out=cf[:, c, :], in_=B2[:, c * 128:(c + 1) * 128],
                     func=AF.Identity, bias=cb[:, c:c + 1])
                P.op('act', 'activation', ['B2', 'cb'], ['sqb'], out=sqb[:, c, :], in_=B2[:, c * 128:(c + 1) * 128],
                     func=AF.Square, bias=cb[:, c:c + 1])
            for c in range(2):
                P.op('pe', 'matmul', ['Bd', 'sqb'], ['B2'], out=B2[:, 256 + c * 128:256 + (c + 1) * 128],
                     lhsT=Bd[:], rhs=sqb[:, c, :], start=True, stop=True)
            P.op('act', 'activation', ['B2'], ['rs'], out=rs[:], in_=B2[:, 256:512].rearrange("p (c t) -> p c t", c=2),
                 func=AF.Ln, bias=EPS)
            P.op('act', 'activation', ['rs'], ['rs'], out=rs[:], in_=rs[:], func=AF.Exp, scale=-0.5)
            for c in range(2):
                P.op('dve', 'scalar_tensor_tensor', ['cf', 'rs', 'cb'], ['cn'], out=cn[:, c, :], in0=cf[:, c, :],
                     scalar=cb[:, 2 + c:3 + c], in1=rs[:, c, :], op0=ALU.mult, op1=ALU.mult)
            emit_sigmoid(P, cn[:], ['cn'], sg[:], 'sg', sg[:], 'sg')
            P.op('dve', 'tensor_tensor', ['cn', 'sg'], ['mixT:c'], out=mixT[:, 0:2, :], in0=cn[:], in1=sg[:], op=ALU.mult)

        if dbg_phase < 1.5:
            continue
        if own:
            proj_tok(C_RQ, 384, B3, 'B3')
            emit_rotary(P, B3, 'B3', cs[s], f'cs{s}', qkr, 'qkr', tmps, 192)
            if dbg_phase < 1.6:
                continue
            P.op('act', 'activation', ['qkr'], ['qkb'], out=qkb[:], in_=qkr[:], func=AF.Copy)
            P.op('dve', 'tensor_tensor', ['qkr', 'kdT'], ['kdec'], out=kdec[:], in0=qkr[:, 192:384], in1=kdT[:], op=ALU.mult)
            for i in range(4):
                P.op('pe', 'transpose', ['qkb', 'ident'], ['B0'], out=tp[0:96, i * 128:(i + 1) * 128],
                     in_=qkb[:, i * 96:(i + 1) * 96], identity=ident[:])
            if dbg_phase < 1.8:
                continue
            P.op('act', 'activation', ['B0'], ['qT'], out=qT[:], in_=tp[:, 0:256].rearrange("p (b t) -> p b t", b=2), func=AF.Copy)
            P.op('act', 'activation', ['B0'], ['kTr'], out=kTr[:], in_=tp[:, 256:512].rearrange("p (b t) -> p b t", b=2), func=AF.Copy)
            if dbg_phase < 1.95:
                continue
            P.op('dve', 'tensor_tensor', ['qT', 'qdT'], ['qdecT'], out=qdecT[:], in0=qT[:], in1=qdT[:], op=ALU.mult)
            if dbg_phase < 2.1:
                continue
            proj_tok(C_RV, 384, B4, 'B4')
            P.op('act', 'activation', ['B4'], ['rvb'], out=rvb[:], in_=B4[:, 0:384], func=AF.Copy)
            sbanks = [(B5, 'B5'), (B6, 'B6'), (B7, 'B7')]
            for h in range(6):
                blk, a = h // 3, h % 3
                bank, bk = sbanks[a]
                P.op('pe', 'matmul', ['kTr', 'qT'], [bk], out=bank[:, blk * 128:(blk + 1) * 128],
                     lhsT=kTr[32 * a:32 * a + 32, blk, :], rhs=qT[32 * a:32 * a + 32, blk, :], start=True, stop=True)
            if dbg_phase < 2.2:
                continue
            for a in range(3):
                bank, bk = sbanks[a]
                P.op('dve', 'tensor_tensor', [bk, 'decT'], ['PTr'], out=PTr[:, a::3, :],
                     in0=bank[:, 0:256].rearrange("p (h t) -> p h t", h=2), in1=decT[:, a::3, :], op=ALU.mult)
            for h in range(6):
                blk, a = h // 3, h % 3
                P.op('pe', 'matmul', ['PTr', 'rvb'], ['B1'], out=B1[:, h * 64:(h + 1) * 64], lhsT=PTr[:, h, :],
                     rhs=rvb[:, h * 64:(h + 1) * 64], start=True, stop=False)
                P.op('pe', 'matmul', ['qdecT', 'Sbf'], ['B1'], out=B1[:, h * 64:(h + 1) * 64],
                     lhsT=qdecT[32 * a:32 * a + 32, blk, :], rhs=Sbf[32 * a:32 * a + 32, blk, :], start=False, stop=True)
            if dbg_phase < 2.3:
                continue
            P.op('pe', 'matmul', ['kdec', 'rvb'], ['B2'], out=B2[0:96, 0:384], lhsT=kdec[:, 0:96], rhs=rvb[:],
                 start=True, stop=True)
            P.op('pe', 'matmul', ['kdec', 'rvb'], ['B7'], out=B7[0:96, 0:384], lhsT=kdec[:, 96:192], rhs=rvb[:],
                 start=True, stop=True)
            for h in range(6):
                blk, a = h // 3, h % 3
                bank, bk = (B2, 'B2') if blk == 0 else (B7, 'B7')
                P.op('dve', 'tensor_tensor', ['Sst', 'G128'], ['Sst'], out=Sst[32 * a:32 * a + 32, blk, :],
                     in0=Sst[32 * a:32 * a + 32, blk, :], in1=G128[32 * a:32 * a + 32, blk, :], op=ALU.mult)
                P.op('dve', 'tensor_tensor', ['Sst', bk], ['Sst'], out=Sst[32 * a:32 * a + 32, blk, :],
                     in0=Sst[32 * a:32 * a + 32, blk, :], in1=bank[32 * a:32 * a + 32, h * 64:(h + 1) * 64], op=ALU.add)
            if dbg_phase < 2.4:
                continue
            head_rstd(B1[:, 0:384], ['B1'], 64)
            P.op('dve', 'tensor_tensor', ['B1', 'r6'], ['f0'], out=v3(f384[0][:]), in0=v3(B1[:, 0:384]), in1=r6b(), op=ALU.mult)
            proj_tok(C_RG, 384, B3, 'B3')
            emit_sigmoid(P, B3[:, 0:384], ['B3'], f384[1][:], 'f1', f384[1][:], 'f1')
            P.op('dve', 'tensor_tensor', ['B3', 'f1'], ['f1'], out=f384[1][:], in0=B3[:, 0:384], in1=f384[1][:], op=ALU.mult)
            P.op('dve', 'tensor_tensor', ['f1', 'Rgb'], ['f1'], out=f384[1][:], in0=f384[1][:], in1=Rgb[:], op=ALU.mult)
            P.op('dve', 'tensor_tensor', ['f0', 'f1'], ['b384'], out=b384[:], in0=f384[0][:], in1=f384[1][:], op=ALU.mult)
            for i in range(3):
                P.op('pe', 'transpose', ['b384', 'ident'], ['B0'], out=tp[:, i * 128:(i + 1) * 128],
                     in_=b384[:, i * 128:(i + 1) * 128], identity=ident[:])
            P.op('act', 'activation', ['B0'], ['mixT:r'], out=mixT[:, 2:5, :],
                 in_=tp[:, 0:384].rearrange("p (c t) -> p c t", c=3), func=AF.Copy)
            P.op('dve', 'tensor_copy', ['Sst'], ['Sbf'], out=Sbf[:], in_=Sst[:])

        if dbg_phase < 3:
            continue
        proj_tok(C_AK, 384, B4, 'B4')
        head_rstd(B4[:, 0:384], ['B4'], 64)
        P.op('dve', 'tensor_tensor', ['B4', 'r6'], ['f2'], out=v3(f384[2][:]), in0=v3(B4[:, 0:384]), in1=r6b(), op=ALU.mult)
        P.op('dve', 'tensor_tensor', ['f2', 'Kgb'], ['b384'], out=b384[:], in0=f384[2][:], in1=Kgb[:], op=ALU.mult)
        for i in range(3):
            P.op('pe', 'transpose', ['b384', 'ident'], ['B0'], out=tp[:, i * 128:(i + 1) * 128],
                 in_=b384[:, i * 128:(i + 1) * 128], identity=ident[:])
        P.op('act', 'activation', ['B0'], [f'KT{slot}'], out=KT[:, :, slot * 128:(slot + 1) * 128],
             in_=tp[:, 0:384].rearrange("p (c t) -> p c t", c=3), func=AF.Copy)
        proj_tok(C_AV, 384, B3, 'B3')
        P.op('act', 'activation', ['B3', 'VA'], [f'VA{slot}'], out=VA[:, slot, :, 0:64], in_=v3(B3[:, 0:384]), func=AF.Copy)
        if not own:
            P.op('dve', 'tensor_copy', ['flg', 'VA'], [f'VA{slot}'], out=VA[:, slot, :, 64:65],
                 in_=flg[:, ti:ti + 1].unsqueeze(1).to_broadcast([128, 6, 1]))
        elif ti >= RING:
            P.op('pool', 'memset', ['VA'], [f'VA{slot}'], ap=VA[:, slot, :, 64:65], constant=1.0)
        if not own:
            continue
        proj_tok(C_AQ, 384, B4, 'B4')
        head_rstd(B4[:, 0:384], ['B4'], 64)
        P.op('dve', 'tensor_tensor', ['B4', 'r6'], ['f2'], out=v3(f384[2][:]), in0=v3(B4[:, 0:384]), in1=r6b(), op=ALU.mult)
        P.op('dve', 'tensor_tensor', ['f2', 'Qgb'], ['b384'], out=b384[:], in0=f384[2][:], in1=Qgb[:], op=ALU.mult)
        for i in range(3):
            P.op('pe', 'transpose', ['b384', 'ident'], ['B0'], out=tp[:, i * 128:(i + 1) * 128],
                 in_=b384[:, i * 128:(i + 1) * 128], identity=ident[:])
        P.op('act', 'activation', ['B0'], ['QT'], out=QT[:], in_=tp[:, 0:384].rearrange("p (c t) -> p c t", c=3), func=AF.Copy)

        if dbg_phase < 4:
            continue
        kts = list(range(ti, ti - 17, -1))
        chunks = [kts[i:i + 4] for i in range(0, 17, 4)]
        ci_glob = 0
        for h in range(6):
            pr, half = h // 2, h % 2
            for cidx, ch in enumerate(chunks):
                bank, bk = (B5, 'B5') if ci_glob % 2 == 0 else (B6, 'B6')
                e_, ek = eb[ci_glob % 2], f'eb{ci_glob % 2}'
                p_, pk_ = PT[ci_glob % 2], f'PT{ci_glob % 2}'
                ci_glob += 1
                n = len(ch) * 128
                for bi, kt in enumerate(ch):
                    ks = kt % RING
                    P.op('pe', 'matmul', [f'KT{ks}', 'QT'], [bk], out=bank[:, bi * 128:(bi + 1) * 128],
                         lhsT=KT[64 * half:64 * half + 64, pr, ks * 128:(ks + 1) * 128],
                         rhs=QT[64 * half:64 * half + 64, pr, :], start=True, stop=True)
                P.op('act', 'activation', [bk], [ek], out=e_[:, 0:n], in_=bank[:, 0:n], func=AF.Exp, scale=0.125)
                off = 128 * (ti - ch[0])
                P.op('dve', 'tensor_tensor', [ek, 'Tab'], [pk_], out=p_[:, 0:n], in0=e_[:, 0:n],
                     in1=Tab[:, h, off:off + n], op=ALU.mult)
                for bi, kt in enumerate(ch):
                    ks = kt % RING
                    P.op('pe', 'matmul', [pk_, f'VA{ks}'], ['B7'], out=B7[:, h * 65:(h + 1) * 65],
                         lhsT=p_[:, bi * 128:(bi + 1) * 128], rhs=VA[:, ks, h, :],
                         start=(kt == kts[0]), stop=(kt == kts[-1]))
        if dbg_phase < 5:
            continue
        b7v = B7[:, 0:390].rearrange("p (h e) -> p h e", h=6)
        P.op('dve', 'reciprocal', ['B7'], ['r6'], out=r6[:].unsqueeze(2), in_=b7v[:, :, 64:65])
        P.op('dve', 'tensor_tensor', ['B7', 'r6'], ['b384'], out=v3(b384[:]), in0=b7v[:, :, 0:64], in1=r6b(), op=ALU.mult)
        for i in range(3):
            P.op('pe', 'transpose', ['b384', 'ident'], ['B0'], out=tp[:, i * 128:(i + 1) * 128],
                 in_=b384[:, i * 128:(i + 1) * 128], identity=ident[:])
        P.op('act', 'activation', ['B0'], ['mixT:a'], out=mixT[:, 5:8, :],
             in_=tp[:, 0:384].rearrange("p (c t) -> p c t", c=3), func=AF.Copy)

        for dh in range(2):
            bank, bk = (B3, 'B3') if dh == 0 else (B4, 'B4')
            for c in range(8):
                P.op('pe', 'matmul', ['mixT:c', 'mixT:r', 'mixT:a', 'Wout'], [bk], out=bank[:],
                     lhsT=mixT[:, c, :], rhs=Wout[:, c, dh * 512:(dh + 1) * 512], start=(c == 0), stop=(c == 7))
            P.op('dve', 'tensor_tensor', [bk, f'xs{s}'], ['ys'], out=ys[:, dh * 512:(dh + 1) * 512], in0=bank[:],
                 in1=xs[s][:, dh * 512:(dh + 1) * 512], op=ALU.add)
        P.dma('sp', 'ys', ['ys'], [f'x1:{t}'], out=x1[t * 128:(t + 1) * 128, :], in_=ys[:])
    return C.finish()


_GAM = 1.0 - 2.0 ** (-5.0 - np.arange(6, dtype=np.float64))


def _t5_bucket(dist):
    nf = np.maximum(dist, 1).astype(np.float32)
    large = 16 + (np.log(nf / np.float32(16)) / np.float32(math.log(2048 / 16)) * np.float32(16)).astype(np.int32)
    large = np.minimum(large, 31)
    return np.where(dist < 16, dist, large)


def _const_tables():
    idn = np.eye(128, dtype=np.float32)
    bdg = np.zeros((128, 128), np.float32)
    bdg[:64, :64] = 1.0 / 64
    bdg[64:, 64:] = 1.0 / 64
    j = np.arange(128)
    diff = j[None, :] - j[:, None]
    dect = np.zeros((128, 6, 128), np.float64)
    for h in range(6):
        dect[:, h, :] = np.where(diff >= 0, _GAM[h] ** np.maximum(diff, 0), 0.0)
    qdt = np.zeros((96, 2, 128), np.float64)
    g128 = np.zeros((96, 2, 64), np.float64)
    for blk in range(2):
        for a in range(3):
            h = 3 * blk + a
            qdt[32 * a:32 * a + 32, blk, :] = (_GAM[h] ** (j + 1.0))[None, :]
            g128[32 * a:32 * a + 32, blk, :] = _GAM[h] ** 128.0
    kdt = np.zeros((128, 192), np.float64)
    for h in range(6):
        kdt[:, 32 * h:32 * h + 32] = (_GAM[h] ** (127.0 - j))[:, None]
    dd = np.arange(PD) - 127
    valid = (dd >= 0) & (dd <= 2048)
    dc = np.clip(dd, 0, 2048)
    mult = ((dc <= 128).astype(np.float64) + ((dc <= 512) & (dc % 4 == 0)) + ((dc <= 2048) & (dc % 16 == 0)))
    bucket = _t5_bucket(dc.astype(np.int32))
    c2 = np.zeros((32, PD), np.float64)
    c2[bucket, np.arange(PD)] = np.where(valid, mult, 0.0)
    f = lambda a: np.ascontiguousarray(a, dtype=np.float32)
    return dict(idn=idn, bdg=bdg, dect=f(dect), qdt=f(qdt), kdt=f(kdt), g128=f(g128), c2=f(c2))


def _rot_tables(pos0, NT):
    T = NT * 128
    pos = (pos0 + np.arange(T)).astype(np.float32)
    inv = (1.0 / (np.float32(10000.0) ** np.linspace(0.0, 1.0, 16, dtype=np.float32))).astype(np.float32)
    ang = (pos[:, None] * inv[None, :]).astype(np.float32).astype(np.float64)
    c, s = np.cos(ang), np.sin(ang)
    sc = 32.0 ** -0.5
    cq = np.tile(c, (1, 6))
    sq = np.tile(s, (1, 6))
    csq = np.stack([np.concatenate([cq, cq * sc], 1), np.concatenate([sq, sq * sc], 1)], 1)
    csk = np.stack([cq * sc, sq * sc], 1)
    loc = np.arange(T, dtype=np.float64)
    dend = np.zeros((T, 192), np.float64)
    for h in range(6):
        dend[:, 32 * h:32 * h + 32] = (_GAM[h] ** (T - 1.0 - loc))[:, None]
    f = lambda a, shp: np.ascontiguousarray(a.reshape(shp), dtype=np.float32)
    return f(csq, (NT, 128, 2, 192)), f(csk, (NT, 128, 2, 96)), f(dend, (NT, 128, 192))


_PROGS = {}


def _prog(kind, NT):
    key = (kind, NT)
    if key not in _PROGS:
        _PROGS[key] = {'p': build_stage_p, 'a': build_stage_a, 'b': build_stage_b}[kind](NT)
    return _PROGS[key]


def _run(nc, in_maps):
    res = run_bass_kernel_spmd(nc, in_maps, core_ids=list(range(len(in_maps))))
    return res.results


def run_model(inp, B, S, NQ, depth):
    NT = S // NQ // 128
    T = NT * 128
    ncore = B * NQ
    ct = _const_tables()
    x = np.ascontiguousarray(inp["x"], dtype=np.float32)
    rot = [_rot_tables((c % NQ) * T, NT) for c in range(ncore)]
    cur = [x[c // NQ, (c % NQ) * T:(c % NQ + 1) * T] for c in range(ncore)]
    f32 = lambda a: np.ascontiguousarray(a, dtype=np.float32)
    for l in range(depth):
        g1 = f32(inp["norm1_g"][l][None])
        win = f32(inp["w_in"][l])
        maps = [dict(xin=f32(cur[c]), g1=g1, win=win, idn=ct["idn"], csk=rot[c][1], dend=rot[c][2])
                for c in range(ncore)]
        outs = _run(_prog('p', NT), maps)
        sin0 = np.zeros((96, ncore, 2, 64), np.float32)
        for r in range(ncore):
            a_r = outs[r]["aout"]
            for blk in range(2):
                for a in range(3):
                    h = 3 * blk + a
                    sin0[32 * a:32 * a + 32, r, blk, :] = a_r[blk, 32 * a:32 * a + 32, 64 * h:64 * h + 64]
        if ncore < 8:
            sin0 = np.concatenate([sin0, np.zeros((96, 8 - ncore, 2, 64), np.float32)], 1)
        full = [np.concatenate([cur[b * NQ + q] for q in range(NQ)], 0) for b in range(B)]
        maps = []
        for c in range(ncore):
            b, q = c // NQ, c % NQ
            start = q * T
            halo = np.zeros((HALO * 128, D), np.float32)
            lo = max(0, start - HALO * 128)
            if start > lo:
                halo[HALO * 128 - (start - lo):] = full[b][lo:start]
            flag = np.zeros((128, HALO), np.float32)
            for i in range(HALO):
                if start - (HALO - i) * 128 >= 0:
                    flag[:, i] = 1.0
            scoef = np.zeros((96, 8, 2, 64), np.float64)
            for r in range(ncore):
                if r // NQ == b and r % NQ < q:
                    for blk in range(2):
                        for a in range(3):
                            scoef[32 * a:32 * a + 32, r, blk, :] = _GAM[3 * blk + a] ** (float(T) * (q - r % NQ - 1))
            cbg = np.concatenate([inp["conv_b"][l].reshape(2, 128).T, inp["conv_g"][l].reshape(2, 128).T], 1)
            maps.append(dict(
                xin=f32(np.concatenate([halo, cur[c]], 0)), g1=g1, win=win, wout=f32(inp["w_out"][l]),
                cwT=f32(inp["conv_w"][l].T), cbg=f32(cbg), retg=f32(inp["ret_g"][l][None]),
                qg=f32(inp["q_g"][l][None]), kg=f32(inp["k_g"][l][None]), relb=f32(inp["rel_bias"]),
                idn=ct["idn"], bdg=ct["bdg"], csq=rot[c][0], dect=ct["dect"], qdt=ct["qdt"], kdt=ct["kdt"],
                g128=ct["g128"], c2=ct["c2"], flag=flag, sin0=sin0, scoef=f32(scoef)))
        outs = _run(_prog('a', NT), maps)
        cur = [outs[c]["x1"] for c in range(ncore)]
        g2 = f32(inp["norm2_g"][l][None])
        w1 = f32(inp["w_ff1"][l])
        w2 = f32(inp["w_ff2"][l])
        maps = [dict(x1=f32(cur[c]), g2=g2, w1=w1, w2=w2, idn=ct["idn"]) for c in range(ncore)]
        outs = _run(_prog('b', NT), maps)
        cur = [outs[c]["x2"] for c in range(ncore)]
    out = np.stack([np.concatenate([cur[b * NQ + q] for q in range(NQ)], 0) for b in range(B)], 0)
    return out.astype(np.float32)


def kernel(**inputs):
    inp = {k: np.asarray(v) for k, v in inputs.items()}
    return run_model(inp, 2, SEQ, NQ, 2)
```

```python
import math
from contextlib import ExitStack

import numpy as np
import concourse.bass as bass
import concourse.mybir as mybir
from concourse.bass_utils import run_bass_kernel_spmd

F32 = mybir.dt.float32
BF16 = mybir.dt.bfloat16
AF = mybir.ActivationFunctionType
ALU = mybir.AluOpType
AX = mybir.AxisListType

D = 1024
DFF = 4096
INW = 2816
EPS = 1e-6
NCORE = 8
NQ = 4
SEQ = 16384
NT_FULL = SEQ // NQ // 128
HALO = 16
RING = 18
PD = 2304
XJ = 2176
C_CONV, C_RQ, C_RK, C_RV, C_RG, C_AQ, C_AK, C_AV = 0, 512, 704, 896, 1280, 1664, 2048, 2432


class Prog:
    ENG = ('pe', 'act', 'dve', 'pool', 'sp')

    def __init__(self):
        self.ops = {e: [] for e in self.ENG}
        self.cnt = {e: 0 for e in self.ENG}
        self.dcnt = {}
        self.state = {}
        self.selfsync = {'pe': False, 'act': True, 'dve': True, 'pool': True, 'sp': False}

    def _deps(self, reads, writes):
        deps = {}

        def add(s, v):
            if deps.get(s, 0) < v:
                deps[s] = v
        for k in reads:
            st = self.state.get(k)
            if st and st[0]:
                add(*st[0])
        for k in writes:
            st = self.state.get(k)
            if st:
                if st[0]:
                    add(*st[0])
                for s, v in st[1].items():
                    add(s, v)
        return deps

    def _commit(self, sv, reads, writes):
        for k in reads:
            st = self.state.setdefault(k, [None, {}])
            if st[1].get(sv[0], 0) < sv[1]:
                st[1][sv[0]] = sv[1]
        for k in writes:
            self.state[k] = [sv, {}]

    def op(self, eng, method, reads=(), writes=(), **kw):
        fn = (lambda m, k: (lambda e: getattr(e, m)(**k)))(method, kw)
        deps = self._deps(reads, writes)
        self.cnt[eng] += 1
        sv = (eng, self.cnt[eng])
        if not self.selfsync[eng]:
            deps.pop(eng, None)
        self.ops[eng].append(('op', fn, deps, None))
        self._commit(sv, reads, writes)

    def dma(self, queue, sem, reads=(), writes=(), **kw):
        fn = (lambda k: (lambda e: e.dma_start(**k)))(kw)
        deps = self._deps(reads, writes)
        key = 'dma:' + sem
        self.dcnt[key] = self.dcnt.get(key, 0) + 16
        sv = (key, self.dcnt[key])
        self.ops[queue].append(('dma', fn, deps, key))
        self._commit(sv, reads, writes)

    def seal(self, sem, keys):
        key = 'dma:' + sem
        for k in keys:
            self.state[k] = [(key, self.dcnt[key]), {}]

    def final_wait_all(self, queue='sp'):
        deps = dict(self.dcnt)
        for e in ('pe', 'act', 'dve', 'pool'):
            if self.cnt[e]:
                deps[e] = self.cnt[e]
        self.ops[queue].append(('wait', None, deps, None))

    def emit(self, nc):
        with ExitStack() as es:
            semh = {}
            for e in ('pe', 'act', 'dve', 'pool'):
                semh[e] = es.enter_context(nc.semaphore('s_' + e))
            for k in self.dcnt:
                semh[k] = es.enter_context(nc.semaphore('s_' + k.replace(':', '_')))
            block = es.enter_context(nc.Block())
            names = {'pe': 'tensor', 'act': 'scalar', 'dve': 'vector', 'pool': 'gpsimd', 'sp': 'sync'}

            def mk(ename):
                ops = self.ops[ename]

                def body(eng):
                    waited = {}
                    for kind, fn, deps, key in ops:
                        for s, v in deps.items():
                            if waited.get(s, 0) >= v:
                                continue
                            eng.wait_ge(semh[s], v)
                            waited[s] = v
                        if kind == 'wait':
                            continue
                        ins = fn(eng)
                        if kind == 'op':
                            ins.then_inc(semh[ename], 1)
                        else:
                            ins.then_inc(semh[key], 16)
                return body
            for ename in self.ENG:
                if self.ops[ename]:
                    getattr(block, names[ename])(mk(ename))


class Ctx:
    def __init__(self):
        self.nc = bass.Bass("TRN2", target_bir_lowering=False)
        self.P = Prog()
        self.es = ExitStack()

    def sb(self, name, shape, dt):
        return self.es.enter_context(self.nc.sbuf_tensor(name, shape, dt))

    def ps(self, name, shape, dt):
        return self.es.enter_context(self.nc.psum_tensor(name, shape, dt))

    def din(self, name, shape, dt=F32):
        return self.nc.dram_tensor(name, list(shape), dt, kind="ExternalInput").ap()

    def dout(self, name, shape, dt=F32):
        return self.nc.dram_tensor(name, list(shape), dt, kind="ExternalOutput").ap()

    def finish(self):
        self.P.final_wait_all('sp')
        self.P.emit(self.nc)
        self.es.close()
        return self.nc


def bcast_rows(ap2d, n):
    return bass.AP(ap2d.tensor, ap2d.offset, [[0, 128], [1, n]])


def emit_norm_T(P, xt, xkey, Gb, xn, junk, ss, rstd, tp, hT_dst, hTkey, ident, tpkeys=('B0',)):
    P.op('act', 'activation', [xkey], ['junk', 'ss'], out=junk[:], in_=xt, func=AF.Square, accum_out=ss[:])
    P.op('act', 'activation', ['ss'], ['rstd'], out=rstd[:], in_=ss[:], func=AF.Ln, scale=1.0 / D, bias=EPS)
    P.op('act', 'activation', ['rstd'], ['rstd'], out=rstd[:], in_=rstd[:], func=AF.Exp, scale=-0.5)
    P.op('dve', 'scalar_tensor_tensor', [xkey, 'rstd', 'Gb'], ['xn'], out=xn[:], in0=xt, scalar=rstd[:, 0:1],
         in1=Gb[:], op0=ALU.mult, op1=ALU.mult)
    for kc in range(8):
        P.op('pe', 'transpose', ['xn', 'ident'], list(tpkeys), out=tp[:, kc * 128:(kc + 1) * 128],
             in_=xn[:, kc * 128:(kc + 1) * 128], identity=ident[:])
    P.op('act', 'activation', list(tpkeys), [hTkey], out=hT_dst, in_=tp[:].rearrange("p (k t) -> p k t", k=8), func=AF.Copy)


def emit_sigmoid(P, src, srckeys, dst, dstkey, tmp, tmpkey):
    P.op('act', 'activation', srckeys, [tmpkey], out=tmp, in_=src, func=AF.Exp, scale=-1.0)
    P.op('act', 'activation', [tmpkey], [tmpkey], out=tmp, in_=tmp, func=AF.Ln, bias=1.0)
    P.op('act', 'activation', [tmpkey], [dstkey], out=dst, in_=tmp, func=AF.Exp, scale=-1.0)


def build_stage_b(NT):
    C = Ctx()
    nc, P = C.nc, C.P
    T = NT * 128
    x1 = C.din("x1", [T, D])
    g2 = C.din("g2", [1, D])
    w1 = C.din("w1", [D, DFF])
    w2 = C.din("w2", [DFF, D])
    idn = C.din("idn", [128, 128])
    x2 = C.dout("x2", [T, D])
    W1 = C.sb("W1", [128, 8, DFF], BF16)
    W2 = C.sb("W2", [128, 32, D], BF16)
    Gb = C.sb("Gb", [128, D], F32)
    ident = C.sb("ident", [128, 128], BF16)
    NX = 4
    xs = [C.sb(f"xs{i}", [128, D], F32) for i in range(NX)]
    xn = C.sb("xn", [128, D], BF16)
    junk = C.sb("junk", [128, D], BF16)
    ss = C.sb("ss", [128, 1], F32)
    rstd = C.sb("rstd", [128, 1], F32)
    hT = [C.sb(f"hT{i}", [128, 8, 256], BF16) for i in range(2)]
    hid = C.sb("hid", [128, 32, 256], BF16)
    rl = [C.sb(f"rl{i}", [128, 512], F32) for i in range(2)]
    ys = [C.sb(f"ys{i}", [128, D], F32) for i in range(2)]
    tp = C.ps("tp", [128, D], BF16)
    ph = [C.ps(f"ph{i}", [128, 512], F32) for i in range(2)]
    po = [C.ps(f"po{i}", [128, 512], F32) for i in range(4)]

    P.dma('pool', 'ident', [], ['ident'], out=ident[:], in_=idn[:, :])
    P.dma('sp', 'Gb', [], ['Gb'], out=Gb[:], in_=bcast_rows(g2, D))
    w1v = w1.rearrange("(kc p) n -> p kc n", p=128)
    w2v = w2.rearrange("(kc p) n -> p kc n", p=128)
    for kc in range(8):
        P.dma('pool', 'W1', [], [f'W1:{kc}'], out=W1[:, kc, :], in_=w1v[:, kc, :])
    for kc in range(32):
        P.dma('pool', 'W2', [], [f'W2:{kc}'], out=W2[:, kc, :], in_=w2v[:, kc, :])
    P.seal('W1', [f'W1:{kc}' for kc in range(8)])
    P.seal('W2', [f'W2:{kc}' for kc in range(32)])

    def load_x(t):
        s = t % NX
        P.dma('sp', f'xs{s}', [], [f'xs{s}'], out=xs[s][:], in_=x1[t * 128:(t + 1) * 128, :])

    def prep(g):
        for i in range(2):
            t = 2 * g + i
            s = t % NX
            emit_norm_T(P, xs[s][:], f'xs{s}', Gb, xn, junk, ss, rstd, tp,
                        hT[g % 2][:, :, i * 128:(i + 1) * 128], f'hT{g % 2}:{i}', ident)

    NG = NT // 2
    for t in range(min(NX, NT)):
        load_x(t)
    prep(0)
    for g in range(NG):
        h = hT[g % 2]
        hk = [f'hT{g % 2}:0', f'hT{g % 2}:1']
        for fp in range(16):
            pb = ph[fp % 2]
            for half in range(2):
                fc = fp * 2 + half
                for kc in range(8):
                    P.op('pe', 'matmul', [f'W1:{kc}'] + hk, [f'ph{fp % 2}'],
                         out=pb[:, half * 256:(half + 1) * 256], lhsT=W1[:, kc, fc * 128:(fc + 1) * 128],
                         rhs=h[:, kc, :], start=(kc == 0), stop=(kc == 7))
            r = rl[fp % 2]
            P.op('act', 'activation', [f'ph{fp % 2}'], [f'rl{fp % 2}'], out=r[:], in_=pb[:], func=AF.Relu)
            P.op('dve', 'tensor_tensor', [f'ph{fp % 2}', f'rl{fp % 2}'], [f'hid:{fp}'],
                 out=hid[:, 2 * fp:2 * fp + 2, :], in0=r[:].rearrange("p (a t) -> p a t", a=2),
                 in1=pb[:].rearrange("p (a t) -> p a t", a=2), op=ALU.mult)
        if g + 1 < NG:
            prep(g + 1)
        for i in range(2):
            t = 2 * g + i
            s = t % NX
            for dh in range(2):
                pi = i * 2 + dh
                pb = po[pi]
                for fc in range(32):
                    P.op('pe', 'matmul', [f'hid:{fc // 2}', f'W2:{fc}'], [f'po{pi}'],
                         out=pb[:], lhsT=hid[:, fc, i * 128:(i + 1) * 128], rhs=W2[:, fc, dh * 512:(dh + 1) * 512],
                         start=(fc == 0), stop=(fc == 31))
                P.op('dve', 'tensor_tensor', [f'po{pi}', f'xs{s}'], [f'ys{i}'],
                     out=ys[i][:, dh * 512:(dh + 1) * 512], in0=pb[:], in1=xs[s][:, dh * 512:(dh + 1) * 512], op=ALU.add)
            P.dma('sp', f'ys{i}', [f'ys{i}'], [f'x2:{t}'], out=x2[t * 128:(t + 1) * 128, :], in_=ys[i][:])
            if t + NX < NT:
                load_x(t + NX)
    return C.finish()


def emit_rotary(P, pt, ptkey, cs, cskey, out, outkey, tmps, n):
    xe, xo = pt[:, 0:2 * n:2], pt[:, 1:2 * n:2]
    c, s = cs[:, 0, 0:n], cs[:, 1, 0:n]
    t1, t2 = tmps[0][:, 0:n], tmps[1][:, 0:n]
    P.op('dve', 'tensor_tensor', [ptkey, cskey], ['rt1'], out=t1, in0=xe, in1=c, op=ALU.mult)
    P.op('dve', 'tensor_tensor', [ptkey, cskey], ['rt2'], out=t2, in0=xo, in1=s, op=ALU.mult)
    P.op('dve', 'tensor_tensor', ['rt1', 'rt2'], [outkey], out=out[:, 0:2 * n:2], in0=t1, in1=t2, op=ALU.subtract)
    P.op('dve', 'tensor_tensor', [ptkey, cskey], ['rt1'], out=t1, in0=xo, in1=c, op=ALU.mult)
    P.op('dve', 'tensor_tensor', [ptkey, cskey], ['rt2'], out=t2, in0=xe, in1=s, op=ALU.mult)
    P.op('dve', 'tensor_tensor', ['rt1', 'rt2'], [outkey], out=out[:, 1:2 * n:2], in0=t1, in1=t2, op=ALU.add)


def build_stage_p(NT):
    C = Ctx()
    nc, P = C.nc, C.P
    T = NT * 128
    xin = C.din("xin", [T, D])
    g1 = C.din("g1", [1, D])
    win = C.din("win", [D, INW])
    idn = C.din("idn", [128, 128])
    csk = C.din("csk", [NT, 128, 2, 96])
    dend = C.din("dend", [NT, 128, 192])
    aout = C.dout("aout", [2, 96, 384])
    Wk = C.sb("Wk", [128, 8, 576], BF16)
    Gb = C.sb("Gb", [128, D], F32)
    ident = C.sb("ident", [128, 128], BF16)
    xs = [C.sb(f"xs{i}", [128, D], F32) for i in range(2)]
    xn = C.sb("xn", [128, D], BF16)
    junk = C.sb("junk", [128, D], BF16)
    ss = C.sb("ss", [128, 1], F32)
    rstd = C.sb("rstd", [128, 1], F32)
    hT = C.sb("hT", [128, 8, 128], BF16)
    cs = [C.sb(f"cs{i}", [128, 2, 96], F32) for i in range(2)]
    de = [C.sb(f"de{i}", [128, 192], F32) for i in range(2)]
    tmps = [C.sb(f"rtmp{i}", [128, 192], F32) for i in range(2)]
    krot = C.sb("krot", [128, 192], F32)
    kdec = C.sb("kdec", [128, 192], BF16)
    vb = C.sb("vb", [128, 384], BF16)
    ao = C.sb("ao", [96, 2, 384], F32)
    tp = C.ps("tp", [128, D], BF16)
    pk = C.ps("pk", [128, 512], F32)
    pv = C.ps("pv", [128, 512], F32)
    pa = [C.ps(f"pa{i}", [128, 512], F32) for i in range(2)]

    P.dma('pool', 'ident', [], ['ident'], out=ident[:], in_=idn[:, :])
    P.dma('sp', 'Gb', [], ['Gb'], out=Gb[:], in_=bcast_rows(g1, D))
    wv = win.rearrange("(kc p) n -> p kc n", p=128)
    for kc in range(8):
        P.dma('pool', 'Wk', [], ['Wk'], out=Wk[:, kc, 0:192], in_=wv[:, kc, C_RK:C_RK + 192])
        P.dma('pool', 'Wk', [], ['Wk'], out=Wk[:, kc, 192:576], in_=wv[:, kc, C_RV:C_RV + 384])
    P.seal('Wk', ['Wk'])

    def load(t):
        s = t % 2
        P.dma('sp', f'xs{s}', [], [f'xs{s}'], out=xs[s][:], in_=xin[t * 128:(t + 1) * 128, :])
        P.dma('sp', f'cs{s}', [], [f'cs{s}'], out=cs[s][:], in_=csk[t])
        P.dma('sp', f'de{s}', [], [f'de{s}'], out=de[s][:], in_=dend[t])

    load(0)
    for t in range(NT):
        s = t % 2
        if t + 1 < NT:
            load(t + 1)
        emit_norm_T(P, xs[s][:], f'xs{s}', Gb, xn, junk, ss, rstd, tp, hT[:], 'hT', ident)
        for kc in range(8):
            P.op('pe', 'matmul', ['Wk', 'hT'], ['pk'], out=pk[:, 0:192], lhsT=hT[:, kc, :], rhs=Wk[:, kc, 0:192],
                 start=(kc == 0), stop=(kc == 7))
        for kc in range(8):
            P.op('pe', 'matmul', ['Wk', 'hT'], ['pv'], out=pv[:, 0:384], lhsT=hT[:, kc, :], rhs=Wk[:, kc, 192:576],
                 start=(kc == 0), stop=(kc == 7))
        emit_rotary(P, pk, 'pk', cs[s], f'cs{s}', krot, 'krot', tmps, 96)
        P.op('dve', 'tensor_tensor', ['krot', f'de{s}'], ['kdec'], out=kdec[:], in0=krot[:], in1=de[s][:], op=ALU.mult)
        P.op('act', 'activation', ['pv'], ['vb'], out=vb[:], in_=pv[:, 0:384], func=AF.Copy)
        for blk in range(2):
            P.op('pe', 'matmul', ['kdec', 'vb'], [f'pa{blk}'], out=pa[blk][0:96, 0:384],
                 lhsT=kdec[:, blk * 96:(blk + 1) * 96], rhs=vb[:], start=(t == 0), stop=(t == NT - 1))
    for blk in range(2):
        P.op('act', 'activation', [f'pa{blk}'], ['ao'], out=ao[:, blk, :], in_=pa[blk][0:96, 0:384], func=AF.Copy)
    P.dma('sp', 'ao', ['ao'], ['aout'], out=aout.rearrange("b p n -> p b n"), in_=ao[:])
    return C.finish()


def build_stage_a(NT, dbg_tiles=None, dbg_phase=9):
    C = Ctx()
    nc, P = C.nc, C.P
    TT = (HALO + NT) * 128
    xin = C.din("xin", [TT, D])
    g1 = C.din("g1", [1, D])
    win = C.din("win", [D, INW])
    wout = C.din("wout", [D, D])
    cwT = C.din("cwT", [256, 31])
    cbg = C.din("cbg", [128, 4])
    retg = C.din("retg", [1, 384])
    qg = C.din("qg", [1, 64])
    kg = C.din("kg", [1, 64])
    relb = C.din("relb", [32, 6])
    idn = C.din("idn", [128, 128])
    bdg = C.din("bdg", [128, 128])
    csq = C.din("csq", [NT, 128, 2, 192])
    dect = C.din("dect", [128, 6, 128])
    qdt = C.din("qdt", [96, 2, 128])
    kdt = C.din("kdt", [128, 192])
    g128 = C.din("g128", [96, 2, 64])
    c2 = C.din("c2", [32, PD])
    flag = C.din("flag", [128, HALO])
    sin0 = C.din("sin0", [96, 8, 2, 64])
    scoef = C.din("scoef", [96, 8, 2, 64])
    rscr = nc.dram_tensor("rscr", [6, 128 * PD], BF16, kind="Internal").ap()
    x1 = C.dout("x1", [NT * 128, D])

    Win = C.sb("Win", [128, 8, INW], BF16)
    Wout = C.sb("Wout", [128, 8, D], BF16)
    Tab = C.sb("Tab", [128, 6, XJ], BF16)
    Dg = C.sb("Dg", [128, 2, 31, 128], BF16)
    KT = C.sb("KT", [128, 3, RING * 128], BF16)
    VA = C.sb("VA", [128, RING, 6, 65], BF16)
    Gb = C.sb("Gb", [128, D], F32)
    Rgb = C.sb("Rgb", [128, 384], F32)
    Qgb = C.sb("Qgb", [128, 384], F32)
    Kgb = C.sb("Kgb", [128, 384], F32)
    ident = C.sb("ident", [128, 128], BF16)
    identf = C.sb("identf", [128, 128], F32)
    Bd = C.sb("Bd", [128, 128], BF16)
    cw = C.sb("cw", [128, 2, 31], F32)
    cb = C.sb("cb", [128, 4], F32)
    decT = C.sb("decT", [128, 6, 128], F32)
    qdT = C.sb("qdT", [128, 2, 128], BF16)
    kdT = C.sb("kdT", [128, 192], F32)
    G128 = C.sb("G128", [96, 2, 64], F32)
    flg = C.sb("flg", [128, HALO], F32)
    xs = [C.sb(f"xs{i}", [128, D], F32) for i in range(2)]
    cs = [C.sb(f"cs{i}", [128, 2, 192], F32) for i in range(2)]
    xn = C.sb("xn", [128, D], BF16)
    junk = C.sb("junk", [128, D], BF16)
    ss = C.sb("ss", [128, 1], F32)
    rstd = C.sb("rstd", [128, 1], F32)
    hT = C.sb("hT", [128, 8, 128], BF16)
    hgl = C.sb("hgl", [128, 2, 160], BF16)
    sg = C.sb("sg", [128, 2, 128], F32)
    cf = C.sb("cf", [128, 2, 128], F32)
    sqb = C.sb("sqb", [128, 2, 128], BF16)
    rs = C.sb("rs", [128, 2, 128], F32)
    cn = C.sb("cn", [128, 2, 128], F32)
    mixT = C.sb("mixT", [128, 8, 128], BF16)
    tmps = [C.sb(f"rtmp{i}", [128, 192], F32) for i in range(2)]
    qkr = C.sb("qkr", [128, 384], F32)
    qkb = C.sb("qkb", [128, 384], BF16)
    kdec = C.sb("kdec", [128, 192], BF16)
    qT = C.sb("qT", [128, 2, 128], BF16)
    kTr = C.sb("kTr", [128, 2, 128], BF16)
    qdecT = C.sb("qdecT", [128, 2, 128], BF16)
    rvb = C.sb("rvb", [128, 384], BF16)
    PTr = C.sb("PTr", [128, 6, 128], BF16)
    Sst = C.sb("Sst", [96, 2, 64], F32)
    Sbf = C.sb("Sbf", [96, 2, 64], BF16)
    f384 = [C.sb(f"f384_{i}", [128, 384], F32) for i in range(3)]
    QT = C.sb("QT", [128, 3, 128], BF16)
    eb = [C.sb(f"eb{i}", [128, 512], BF16) for i in range(2)]
    PT = [C.sb(f"PT{i}", [128, 512], BF16) for i in range(2)]
    ys = C.sb("ys", [128, D], F32)
    Esb = C.sb("Esb", [32, 6], F32)
    B = [C.ps(f"B{i}", [128, 512], F32) for i in range(1, 8)]
    B0 = C.ps("B0", [128, D], BF16)
    tp = B0
    B1, B2, B3, B4, B5, B6, B7 = B

    P.dma('pool', 'c_ident', [], ['ident'], out=ident[:], in_=idn[:, :])
    P.dma('sp', 'c_identf', [], ['identf'], out=identf[:], in_=idn[:, :])
    P.dma('pool', 'c_bd', [], ['Bd'], out=Bd[:], in_=bdg[:, :])
    P.dma('sp', 'c_Gb', [], ['Gb'], out=Gb[:], in_=bcast_rows(g1, D))
    P.dma('sp', 'c_Rgb', [], ['Rgb'], out=Rgb[:], in_=bcast_rows(retg, 384))
    P.dma('sp', 'c_Qgb', [], ['Qgb'], out=Qgb[:].rearrange("p (h d) -> p h d", h=6),
          in_=bass.AP(qg.tensor, 0, [[0, 128], [0, 6], [1, 64]]))
    P.dma('sp', 'c_Kgb', [], ['Kgb'], out=Kgb[:].rearrange("p (h d) -> p h d", h=6),
          in_=bass.AP(kg.tensor, 0, [[0, 128], [0, 6], [1, 64]]))
    P.dma('sp', 'c_cw', [], ['cw'], out=cw[:], in_=cwT.rearrange("(c p) j -> p c j", p=128))
    P.dma('sp', 'c_cb', [], ['cb'], out=cb[:], in_=cbg[:, :])
    P.dma('sp', 'c_decT', [], ['decT'], out=decT[:], in_=dect[:, :, :])
    P.op('pool', 'memset', [], ['qdT'], ap=qdT[:].rearrange("p b t -> p (b t)"), constant=0.0)
    P.dma('pool', 'c_qdT', [], ['qdT'], out=qdT[0:96, :, :], in_=qdt[:, :, :])
    P.dma('sp', 'c_kdT', [], ['kdT'], out=kdT[:], in_=kdt[:, :])
    P.dma('sp', 'c_g128', [], ['G128'], out=G128[:], in_=g128[:, :, :])
    P.dma('sp', 'c_flag', [], ['flg'], out=flg[:], in_=flag[:, :])
    P.dma('sp', 'c_E', [], ['Esb'], out=Esb[:], in_=relb[:, :])
    wv = win.rearrange("(kc p) n -> p kc n", p=128)
    wov = wout.rearrange("(kc p) n -> p kc n", p=128)
    for kc in range(8):
        P.dma('pool', 'Win', [], ['Win'], out=Win[:, kc, :], in_=wv[:, kc, :])
    for kc in range(8):
        P.dma('pool', 'Wout', [], ['Wout'], out=Wout[:, kc, :], in_=wov[:, kc, :])
    P.seal('Win', ['Win'])
    P.seal('Wout', ['Wout'])

    P.op('act', 'activation', ['Esb'], ['Esb'], out=Esb[:], in_=Esb[:], func=AF.Exp)
    nch = (PD + 511) // 512
    ktkeys = [f'KT{i}' for i in range(RING)]
    Fsb = KT[0:6, 0, :]
    for ci in range(nch):
        lo = ci * 512
        n = min(512, PD - lo)
        P.dma('sp', 'c_C2', [], ['ys'], out=ys[0:32, 0:n], in_=c2[:, lo:lo + n])
        P.op('pe', 'matmul', ['Esb', 'ys'], ['B1'], out=B1[0:6, 0:n], lhsT=Esb[:], rhs=ys[0:32, 0:n],
             start=True, stop=True)
        P.op('act', 'activation', ['B1'], ktkeys, out=KT[0:6, 0, lo:lo + n], in_=B1[0:6, 0:n], func=AF.Copy)
    P.dma('sp', 'c_rscr', ktkeys, ['rscr'], out=rscr.rearrange("h (r n) -> h r n", r=128),
          in_=bass.AP(Fsb.tensor, Fsb.offset, [[3 * RING * 128, 6], [0, 128], [1, PD]]))
    for h in range(6):
        P.dma('sp', 'c_Tab', ['rscr'], ['Tab'], out=Tab[:, h, :],
              in_=bass.AP(rscr.tensor, h * 128 * PD + 127, [[PD - 1, 128], [1, XJ]]))
    P.seal('c_Tab', ['Tab'])
    for c in range(2):
        for j in range(31):
            P.op('pool', 'tensor_scalar', ['identf', 'cw'], ['Dg'], out=Dg[:, c, j, :], in0=identf[:],
                 scalar1=cw[:, c, j:j + 1], scalar2=None, op0=ALU.mult)
    P.op('pool', 'memset', [], ['VA'], ap=VA[:].rearrange("p r h e -> p (r h e)"), constant=1.0)
    big = xs[0][0:96, :].rearrange("p (r x) -> p r x", r=8)
    big2 = xs[1][0:96, :].rearrange("p (r x) -> p r x", r=8)
    P.dma('sp', 'c_s0', [], ['xs0'], out=big, in_=sin0.rearrange("p r b e -> p r (b e)"))
    P.dma('sp', 'c_s1', [], ['xs1'], out=big2, in_=scoef.rearrange("p r b e -> p r (b e)"))
    P.op('dve', 'tensor_tensor', ['xs0', 'xs1'], ['xs0'], out=big, in0=big, in1=big2, op=ALU.mult)
    P.op('dve', 'tensor_reduce', ['xs0'], ['Sst'], out=Sst[:].rearrange("p b e -> p (b e)"),
         in_=big.rearrange("p r x -> p x r"), axis=AX.X, op=ALU.add)
    P.op('dve', 'tensor_copy', ['Sst'], ['Sbf'], out=Sbf[:], in_=Sst[:])

    def load(ti):
        s = ti % 2
        P.dma('sp', f'xs{s}', [], [f'xs{s}'], out=xs[s][:], in_=xin[ti * 128:(ti + 1) * 128, :])
        t = ti - HALO
        if t >= 0:
            P.dma('sp', f'cs{s}', [], [f'cs{s}'], out=cs[s][:], in_=csq[t])

    def proj_tok(col, n, bank, bkey):
        for kc in range(8):
            P.op('pe', 'matmul', ['Win', 'hT'], [bkey], out=bank[:, 0:n], lhsT=hT[:, kc, :],
                 rhs=Win[:, kc, col:col + n], start=(kc == 0), stop=(kc == 7))

    def head_rstd(src, srckeys, n_over):
        P.op('act', 'activation', srckeys, ['sq384'], out=f384[3][:], in_=src, func=AF.Square)
        P.op('dve', 'tensor_reduce', ['sq384'], ['s6'], out=s6[:], in_=f384[3][:].rearrange("p (h d) -> p h d", h=6),
             axis=AX.X, op=ALU.add)
        P.op('act', 'activation', ['s6'], ['r6'], out=r6[:], in_=s6[:], func=AF.Ln, scale=1.0 / n_over, bias=EPS)
        P.op('act', 'activation', ['r6'], ['r6'], out=r6[:], in_=r6[:], func=AF.Exp, scale=-0.5)

    def r6b():
        return r6[:].unsqueeze(2).to_broadcast([128, 6, 64])

    def v3(ap):
        return ap.rearrange("p (h d) -> p h d", h=6)

    sqj = {k: C.sb(f"sqj_{k}", [128, 384], BF16) for k in ('r', 'k', 'q')}
    s6c = {k: C.sb(f"s6_{k}", [128, 6], F32) for k in ('r', 'k', 'q')}
    r6c = {k: C.sb(f"r6_{k}", [128, 6], F32) for k in ('r', 'k', 'q', 'a')}
    bo = {k: C.sb(f"bo_{k}", [128, 384], BF16) for k in ('r', 'k', 'q', 'a')}
    BK = {'B1': B1, 'B2': B2, 'B3': B3, 'B4': B4, 'B5': B5, 'B6': B6, 'B7': B7}
    B0K = ['B0a', 'B0b', 'B0c']

    def proj(col, n, bk):
        bank = BK[bk]
        for kc in range(8):
            P.op('pe', 'matmul', ['Win', 'hT'], [bk], out=bank[:, 0:n], lhsT=hT[:, kc, :],
                 rhs=Win[:, kc, col:col + n], start=(kc == 0), stop=(kc == 7))

    def rstd6(tag, src, srckeys):
        P.op('act', 'activation', srckeys, ['sqj' + tag], out=sqj[tag][:], in_=src, func=AF.Square)
        P.op('dve', 'tensor_reduce', ['sqj' + tag], ['s6' + tag], out=s6c[tag][:],
             in_=sqj[tag][:].rearrange("p (h d) -> p h d", h=6), axis=AX.X, op=ALU.add)
        P.op('act', 'activation', ['s6' + tag], ['r6' + tag], out=r6c[tag][:], in_=s6c[tag][:], func=AF.Ln,
             scale=1.0 / 64, bias=EPS)
        P.op('act', 'activation', ['r6' + tag], ['r6' + tag], out=r6c[tag][:], in_=r6c[tag][:], func=AF.Exp, scale=-0.5)

    def rb(tag):
        return r6c[tag][:].unsqueeze(2).to_broadcast([128, 6, 64])

    def tr3(src, srckey, c0, dst, dstkey):
        keys = sorted({B0K[(c0 + i * 128) // 384] for i in range(3)} | {B0K[(c0 + i * 128 + 127) // 384] for i in range(3)})
        for i in range(3):
            P.op('pe', 'transpose', [srckey, 'ident'], keys, out=tp[:, c0 + i * 128:c0 + (i + 1) * 128],
                 in_=src[:, i * 128:(i + 1) * 128], identity=ident[:])
        P.op('act', 'activation', keys, [dstkey], out=dst,
             in_=tp[:, c0:c0 + 384].rearrange("p (c t) -> p c t", c=3), func=AF.Copy)

    load(0)
    for ti in range(HALO + NT):
        t = ti - HALO
        s = ti % 2
        slot = ti % RING
        if ti + 1 < HALO + NT:
            load(ti + 1)
        own = t >= 0
        if dbg_tiles is not None and ti >= dbg_tiles:
            continue
        emit_norm_T(P, xs[s][:], f'xs{s}', Gb, xn, junk, ss, rstd, tp, hT[:], 'hT', ident, tpkeys=B0K)

        if own or t == -1:
            for cc in range(4):
                for kc in range(8):
                    P.op('pe', 'matmul', ['Win', 'hT'], ['B1'], out=B1[:, cc * 128:(cc + 1) * 128],
                         lhsT=Win[:, kc, cc * 128:(cc + 1) * 128], rhs=hT[:, kc, :], start=(kc == 0), stop=(kc == 7))
        if own:
            proj(C_RQ, 384, 'B3')
            proj(C_RV, 384, 'B4')
        proj(C_AK, 384, 'B7')
        if not own:
            proj(C_AV, 384, 'B6')

        if own or t == -1:
            if own:
                P.op('pool', 'tensor_copy', ['hgl'], ['hgl'], out=hgl[:, :, 0:30], in_=hgl[:, :, 128:158])
            emit_sigmoid(P, B1[:, 256:512].rearrange("p (c t) -> p c t", c=2), ['B1'], sg[:], 'sg', sg[:], 'sg')
            if own:
                P.op('dve', 'tensor_tensor', ['B1', 'sg'], ['hgl'], out=hgl[:, :, 30:158],
                     in0=B1[:, 0:256].rearrange("p (c t) -> p c t", c=2), in1=sg[:], op=ALU.mult)
            else:
                P.op('dve', 'scalar_tensor_tensor', ['B1', 'sg', 'flg'], ['hgl'], out=hgl[:, :, 30:158],
                     in0=B1[:, 0:256].rearrange("p (c t) -> p c t", c=2), scalar=flg[:, ti:ti + 1], in1=sg[:],
                     op0=ALU.mult, op1=ALU.mult)

        rstd6('k', B7[:, 0:384], ['B7'])
        P.op('dve', 'tensor_tensor', ['B7', 'r6k'], ['f2'], out=v3(f384[2][:]), in0=v3(B7[:, 0:384]), in1=rb('k'), op=ALU.mult)
        P.op('dve', 'tensor_tensor', ['f2', 'Kgb'], ['bok'], out=bo['k'][:], in0=f384[2][:], in1=Kgb[:], op=ALU.mult)

        if not own:
            tr3(bo['k'], 'bok', 0, KT[:, :, slot * 128:(slot + 1) * 128], f'KT{slot}')
            P.op('act', 'activation', ['B6', 'VA', 'flg'], [f'VA{slot}'], out=VA[:, slot, :, 0:64], in_=v3(B6[:, 0:384]),
                 func=AF.Copy, scale=flg[:, ti:ti + 1])
            P.op('dve', 'tensor_copy', ['flg', 'VA'], [f'VA{slot}'], out=VA[:, slot, :, 64:65],
                 in_=flg[:, ti:ti + 1].unsqueeze(1).to_broadcast([128, 6, 1]))
            continue

        emit_rotary(P, B3, 'B3', cs[s], f'cs{s}', qkr, 'qkr', tmps, 192)
        P.op('act', 'activation', ['qkr'], ['qkb'], out=qkb[:], in_=qkr[:], func=AF.Copy)
        P.op('dve', 'tensor_tensor', ['qkr', 'kdT'], ['kdec'], out=kdec[:], in0=qkr[:, 192:384], in1=kdT[:], op=ALU.mult)
        P.op('act', 'activation', ['B4'], ['rvb'], out=rvb[:], in_=B4[:, 0:384], func=AF.Copy)
        for c in range(2):
            for j in range(31):
                P.op('pe', 'matmul', ['Dg', 'hgl'], ['B2'], out=B2[:, c * 128:(c + 1) * 128],
                     lhsT=Dg[:, c, j, :], rhs=hgl[:, c, j:j + 128], start=(j == 0), stop=(j == 30))
        proj(C_RG, 384, 'B1')
        proj(C_AV, 384, 'B6')
        for i in range(4):
            P.op('pe', 'transpose', ['qkb', 'ident'], ['B0a', 'B0b'], out=tp[0:96, i * 128:(i + 1) * 128],
                 in_=qkb[:, i * 96:(i + 1) * 96], identity=ident[:])
        P.op('act', 'activation', ['B0a', 'B0b'], ['qT'], out=qT[:], in_=tp[:, 0:256].rearrange("p (b t) -> p b t", b=2), func=AF.Copy)
        P.op('act', 'activation', ['B0a', 'B0b'], ['kTr'], out=kTr[:], in_=tp[:, 256:512].rearrange("p (b t) -> p b t", b=2), func=AF.Copy)
        P.op('dve', 'tensor_tensor', ['qT', 'qdT'], ['qdecT'], out=qdecT[:], in0=qT[:], in1=qdT[:], op=ALU.mult)
        for c in range(2):
            P.op('act', 'activation', ['B2', 'cb'], ['cf'], out=cf[:, c, :], in_=B2[:, c * 128:(c + 1) * 128],
                 func=AF.Identity, bias=cb[:, c:c + 1])
            P.op('act', 'activation', ['B2', 'cb'], ['sqb'], out=sqb[:, c, :], in_=B2[:, c * 128:(c + 1) * 128],
                 func=AF.Square, bias=cb[:, c:c + 1])
        P.op('act', 'activation', ['B6', 'VA'], [f'VA{slot}'], out=VA[:, slot, :, 0:64], in_=v3(B6[:, 0:384]), func=AF.Copy)
        if ti >= RING:
            P.op('pool', 'memset', ['VA'], [f'VA{slot}'], ap=VA[:, slot, :, 64:65], constant=1.0)
        sbanks = [(B5, 'B5'), (B4, 'B4'), (B3, 'B3')]
        for h in range(6):
            blk, a = h // 3, h % 3
            bank, bk = sbanks[a]
            P.op('pe', 'matmul', ['kTr', 'qT'], [bk], out=bank[:, blk * 128:(blk + 1) * 128],
                 lhsT=kTr[32 * a:32 * a + 32, blk, :], rhs=qT[32 * a:32 * a + 32, blk, :], start=True, stop=True)
        for c in range(2):
            P.op('pe', 'matmul', ['Bd', 'sqb'], ['B2'], out=B2[:, 256 + c * 128:256 + (c + 1) * 128],
                 lhsT=Bd[:], rhs=sqb[:, c, :], start=True, stop=True)
        proj(C_AQ, 384, 'B6')
        for a in range(3):
            bank, bk = sbanks[a]
            P.op('dve', 'tensor_tensor', [bk, 'decT'], ['PTr'], out=PTr[:, a::3, :],
                 in0=bank[:, 0:256].rearrange("p (h t) -> p h t", h=2), in1=decT[:, a::3, :], op=ALU.mult)
        P.op('act', 'activation', ['B2'], ['rs'], out=rs[:], in_=B2[:, 256:512].rearrange("p (c t) -> p c t", c=2),
             func=AF.Ln, bias=EPS)
        P.op('act', 'activation', ['rs'], ['rs'], out=rs[:], in_=rs[:], func=AF.Exp, scale=-0.5)
        for h in range(6):
            blk, a = h // 3, h % 3
            P.op('pe', 'matmul', ['PTr', 'rvb'], ['B5'], out=B5[:, h * 64:(h + 1) * 64], lhsT=PTr[:, h, :],
                 rhs=rvb[:, h * 64:(h + 1) * 64], start=True, stop=False)
            P.op('pe', 'matmul', ['qdecT', 'Sbf'], ['B5'], out=B5[:, h * 64:(h + 1) * 64],
                 lhsT=qdecT[32 * a:32 * a + 32, blk, :], rhs=Sbf[32 * a:32 * a + 32, blk, :], start=False, stop=True)
        P.op('pe', 'matmul', ['kdec', 'rvb'], ['B4'], out=B4[0:96, 0:384], lhsT=kdec[:, 0:96], rhs=rvb[:], start=True, stop=True)
        P.op('pe', 'matmul', ['kdec', 'rvb'], ['B3'], out=B3[0:96, 0:384], lhsT=kdec[:, 96:192], rhs=rvb[:], start=True, stop=True)
        tr3(bo['k'], 'bok', 512, KT[:, :, slot * 128:(slot + 1) * 128], f'KT{slot}')
        for c in range(2):
            P.op('dve', 'scalar_tensor_tensor', ['cf', 'rs', 'cb'], ['cn'], out=cn[:, c, :], in0=cf[:, c, :],
                 scalar=cb[:, 2 + c:3 + c], in1=rs[:, c, :], op0=ALU.mult, op1=ALU.mult)
        emit_sigmoid(P, cn[:], ['cn'], sg[:], 'sg', sg[:], 'sg')
        P.op('dve', 'tensor_tensor', ['cn', 'sg'], ['mixT:c'], out=mixT[:, 0:2, :], in0=cn[:], in1=sg[:], op=ALU.mult)
        rstd6('q', B6[:, 0:384], ['B6'])
        P.op('dve', 'tensor_tensor', ['B6', 'r6q'], ['f2'], out=v3(f384[2][:]), in0=v3(B6[:, 0:384]), in1=rb('q'), op=ALU.mult)
        P.op('dve', 'tensor_tensor', ['f2', 'Qgb'], ['boq'], out=bo['q'][:], in0=f384[2][:], in1=Qgb[:], op=ALU.mult)
        tr3(bo['q'], 'boq', 0, QT[:], 'QT')
        for h in range(6):
            blk, a = h // 3, h % 3
            bank, bk = (B4, 'B4') if blk == 0 else (B3, 'B3')
            P.op('dve', 'tensor_tensor', ['Sst', 'G128'], ['Sst'], out=Sst[32 * a:32 * a + 32, blk, :],
                 in0=Sst[32 * a:32 * a + 32, blk, :], in1=G128[32 * a:32 * a + 32, blk, :], op=ALU.mult)
            P.op('dve', 'tensor_tensor', ['Sst', bk], ['Sst'], out=Sst[32 * a:32 * a + 32, blk, :],
                 in0=Sst[32 * a:32 * a + 32, blk, :], in1=bank[32 * a:32 * a + 32, h * 64:(h + 1) * 64], op=ALU.add)
        rstd6('r', B5[:, 0:384], ['B5'])
        P.op('dve', 'tensor_tensor', ['B5', 'r6r'], ['f0'], out=v3(f384[0][:]), in0=v3(B5[:, 0:384]), in1=rb('r'), op=ALU.mult)
        emit_sigmoid(P, B1[:, 0:384], ['B1'], f384[1][:], 'f1', f384[1][:], 'f1')
        P.op('dve', 'tensor_tensor', ['B1', 'f1'], ['f1'], out=f384[1][:], in0=B1[:, 0:384], in1=f384[1][:], op=ALU.mult)
        P.op('dve', 'tensor_tensor', ['f1', 'Rgb'], ['f1'], out=f384[1][:], in0=f384[1][:], in1=Rgb[:], op=ALU.mult)
        P.op('dve', 'tensor_tensor', ['f0', 'f1'], ['bor'], out=bo['r'][:], in0=f384[0][:], in1=f384[1][:], op=ALU.mult)
        P.op('dve', 'tensor_copy', ['Sst'], ['Sbf'], out=Sbf[:], in_=Sst[:])

        kts = list(range(ti, ti - 17, -1))
        chunks = [kts[i:i + 4] for i in range(0, 17, 4)]
        seq = [(h, ch) for h in range(6) for ch in chunks]
        sb2 = [(B3, 'B3'), (B4, 'B4')]

        def emit_st(i):
            h, ch = seq[i]
            pr, half = h // 2, h % 2
            bank, bk = sb2[i % 2]
            for bi, kt in enumerate(ch):
                ks = kt % RING
                P.op('pe', 'matmul', [f'KT{ks}', 'QT'], [bk], out=bank[:, bi * 128:(bi + 1) * 128],
                     lhsT=KT[64 * half:64 * half + 64, pr, ks * 128:(ks + 1) * 128],
                     rhs=QT[64 * half:64 * half + 64, pr, :], start=True, stop=True)

        emit_st(0)
        ret_T_done = False
        for i, (h, ch) in enumerate(seq):
            bank, bk = sb2[i % 2]
            e_, ek = eb[i % 2], f'eb{i % 2}'
            p_, pk_ = PT[i % 2], f'PT{i % 2}'
            n = len(ch) * 128
            if i + 1 < len(seq):
                emit_st(i + 1)
            if not ret_T_done and i == 1:
                tr3(bo['r'], 'bor', 384, mixT[:, 2:5, :], 'mixT:r')
                ret_T_done = True
            P.op('act', 'activation', [bk], [ek], out=e_[:, 0:n], in_=bank[:, 0:n], func=AF.Exp, scale=0.125)
            off = 128 * (ti - ch[0])
            P.op('dve', 'tensor_tensor', [ek, 'Tab'], [pk_], out=p_[:, 0:n], in0=e_[:, 0:n],
                 in1=Tab[:, h, off:off + n], op=ALU.mult)
            for bi, kt in enumerate(ch):
                ks = kt % RING
                P.op('pe', 'matmul', [pk_, f'VA{ks}'], ['B7'], out=B7[:, h * 65:(h + 1) * 65],
                     lhsT=p_[:, bi * 128:(bi + 1) * 128], rhs=VA[:, ks, h, :],
                     start=(kt == kts[0]), stop=(kt == kts[-1]))
        b7v = B7[:, 0:390].rearrange("p (h e) -> p h e", h=6)
        P.op('dve', 'reciprocal', ['B7'], ['r6a'], out=r6c['a'][:].unsqueeze(2), in_=b7v[:, :, 64:65])
        P.op('dve', 'tensor_tensor', ['B7', 'r6a'], ['boa'], out=v3(bo['a'][:]), in0=b7v[:, :, 0:64], in1=rb('a'), op=ALU.mult)
        tr3(bo['a'], 'boa', 0, mixT[:, 5:8, :], 'mixT:a')

        for dh in range(2):
            bank, bk = (B1, 'B1') if dh == 0 else (B2, 'B2')
            for c in range(8):
                P.op('pe', 'matmul', ['mixT:c', 'mixT:r', 'mixT:a', 'Wout'], [bk], out=bank[:],
                     lhsT=mixT[:, c, :], rhs=Wout[:, c, dh * 512:(dh + 1) * 512], start=(c == 0), stop=(c == 7))
            P.op('dve', 'tensor_tensor', [bk, f'xs{s}'], ['ys'], out=ys[:, dh * 512:(dh + 1) * 512], in0=bank[:],
                 in1=xs[s][:, dh * 512:(dh + 1) * 512], op=ALU.add)
        P.dma('sp', 'ys', ['ys'], [f'x1:{t}'], out=x1[t * 128:(t + 1) * 128, :], in_=ys[:])
    return C.finish()


_GAM = 1.0 - 2.0 ** (-5.0 - np.arange(6, dtype=np.float64))


def _t5_bucket(dist):
    nf = np.maximum(dist, 1).astype(np.float32)
    large = 16 + (np.log(nf / np.float32(16)) / np.float32(math.log(2048 / 16)) * np.float32(16)).astype(np.int32)
    large = np.minimum(large, 31)
    return np.where(dist < 16, dist, large)


def _const_tables():
    idn = np.eye(128, dtype=np.float32)
    bdg = np.zeros((128, 128), np.float32)
    bdg[:64, :64] = 1.0 / 64
    bdg[64:, 64:] = 1.0 / 64
    j = np.arange(128)
    diff = j[None, :] - j[:, None]
    dect = np.zeros((128, 6, 128), np.float64)
    for h in range(6):
        dect[:, h, :] = np.where(diff >= 0, _GAM[h] ** np.maximum(diff, 0), 0.0)
    qdt = np.zeros((96, 2, 128), np.float64)
    g128 = np.zeros((96, 2, 64), np.float64)
    for blk in range(2):
        for a in range(3):
            h = 3 * blk + a
            qdt[32 * a:32 * a + 32, blk, :] = (_GAM[h] ** (j + 1.0))[None, :]
            g128[32 * a:32 * a + 32, blk, :] = _GAM[h] ** 128.0
    kdt = np.zeros((128, 192), np.float64)
    for h in range(6):
        kdt[:, 32 * h:32 * h + 32] = (_GAM[h] ** (127.0 - j))[:, None]
    dd = np.arange(PD) - 127
    valid = (dd >= 0) & (dd <= 2048)
    dc = np.clip(dd, 0, 2048)
    mult = ((dc <= 128).astype(np.float64) + ((dc <= 512) & (dc % 4 == 0)) + ((dc <= 2048) & (dc % 16 == 0)))
    bucket = _t5_bucket(dc.astype(np.int32))
    c2 = np.zeros((32, PD), np.float64)
    c2[bucket, np.arange(PD)] = np.where(valid, mult, 0.0)
    f = lambda a: np.ascontiguousarray(a, dtype=np.float32)
    return dict(idn=idn, bdg=bdg, dect=f(dect), qdt=f(qdt), kdt=f(kdt), g128=f(g128), c2=f(c2))


def _rot_tables(pos0, NT):
    T = NT * 128
    pos = (pos0 + np.arange(T)).astype(np.float32)
    inv = (1.0 / (np.float32(10000.0) ** np.linspace(0.0, 1.0, 16, dtype=np.float32))).astype(np.float32)
    ang = (pos[:, None] * inv[None, :]).astype(np.float32).astype(np.float64)
    c, s = np.cos(ang), np.sin(ang)
    sc = 32.0 ** -0.5
    cq = np.tile(c, (1, 6))
    sq = np.tile(s, (1, 6))
    csq = np.stack([np.concatenate([cq, cq * sc], 1), np.concatenate([sq, sq * sc], 1)], 1)
    csk = np.stack([cq * sc, sq * sc], 1)
    loc = np.arange(T, dtype=np.float64)
    dend = np.zeros((T, 192), np.float64)
    for h in range(6):
        dend[:, 32 * h:32 * h + 32] = (_GAM[h] ** (T - 1.0 - loc))[:, None]
    f = lambda a, shp: np.ascontiguousarray(a.reshape(shp), dtype=np.float32)
    return f(csq, (NT, 128, 2, 192)), f(csk, (NT, 128, 2, 96)), f(dend, (NT, 128, 192))


_PROGS = {}


def _prog(kind, NT):
    key = (kind, NT)
    if key not in _PROGS:
        _PROGS[key] = {'p': build_stage_p, 'a': build_stage_a, 'b': build_stage_b}[kind](NT)
    return _PROGS[key]


def _run(nc, in_maps):
    res = run_bass_kernel_spmd(nc, in_maps, core_ids=list(range(len(in_maps))))
    return res.results


def run_model(inp, B, S, NQ, depth):
    NT = S // NQ // 128
    T = NT * 128
    ncore = B * NQ
    ct = _const_tables()
    x = np.ascontiguousarray(inp["x"], dtype=np.float32)
    rot = [_rot_tables((c % NQ) * T, NT) for c in range(ncore)]
    cur = [x[c // NQ, (c % NQ) * T:(c % NQ + 1) * T] for c in range(ncore)]
    f32 = lambda a: np.ascontiguousarray(a, dtype=np.float32)
    for l in range(depth):
        g1 = f32(inp["norm1_g"][l][None])
        win = f32(inp["w_in"][l])
        maps = [dict(xin=f32(cur[c]), g1=g1, win=win, idn=ct["idn"], csk=rot[c][1], dend=rot[c][2])
                for c in range(ncore)]
        outs = _run(_prog('p', NT), maps)
        sin0 = np.zeros((96, ncore, 2, 64), np.float32)
        for r in range(ncore):
            a_r = outs[r]["aout"]
            for blk in range(2):
                for a in range(3):
                    h = 3 * blk + a
                    sin0[32 * a:32 * a + 32, r, blk, :] = a_r[blk, 32 * a:32 * a + 32, 64 * h:64 * h + 64]
        if ncore < 8:
            sin0 = np.concatenate([sin0, np.zeros((96, 8 - ncore, 2, 64), np.float32)], 1)
        full = [np.concatenate([cur[b * NQ + q] for q in range(NQ)], 0) for b in range(B)]
        maps = []
        for c in range(ncore):
            b, q = c // NQ, c % NQ
            start = q * T
            halo = np.zeros((HALO * 128, D), np.float32)
            lo = max(0, start - HALO * 128)
            if start > lo:
                halo[HALO * 128 - (start - lo):] = full[b][lo:start]
            flag = np.zeros((128, HALO), np.float32)
            for i in range(HALO):
                if start - (HALO - i) * 128 >= 0:
                    flag[:, i] = 1.0
            scoef = np.zeros((96, 8, 2, 64), np.float64)
            for r in range(ncore):
                if r // NQ == b and r % NQ < q:
                    for blk in range(2):
                        for a in range(3):
                            scoef[32 * a:32 * a + 32, r, blk, :] = _GAM[3 * blk + a] ** (float(T) * (q - r % NQ - 1))
            cbg = np.concatenate([inp["conv_b"][l].reshape(2, 128).T, inp["conv_g"][l].reshape(2, 128).T], 1)
            maps.append(dict(
                xin=f32(np.concatenate([halo, cur[c]], 0)), g1=g1, win=win, wout=f32(inp["w_out"][l]),
                cwT=f32(inp["conv_w"][l].T), cbg=f32(cbg), retg=f32(inp["ret_g"][l][None]),
                qg=f32(inp["q_g"][l][None]), kg=f32(inp["k_g"][l][None]), relb=f32(inp["rel_bias"]),
                idn=ct["idn"], bdg=ct["bdg"], csq=rot[c][0], dect=ct["dect"], qdt=ct["qdt"], kdt=ct["kdt"],
                g128=ct["g128"], c2=ct["c2"], flag=flag, sin0=sin0, scoef=f32(scoef)))
        outs = _run(_prog('a', NT), maps)
        cur = [outs[c]["x1"] for c in range(ncore)]
        g2 = f32(inp["norm2_g"][l][None])
        w1 = f32(inp["w_ff1"][l])
        w2 = f32(inp["w_ff2"][l])
        maps = [dict(x1=f32(cur[c]), g2=g2, w1=w1, w2=w2, idn=ct["idn"]) for c in range(ncore)]
        outs = _run(_prog('b', NT), maps)
        cur = [outs[c]["x2"] for c in range(ncore)]
    out = np.stack([np.concatenate([cur[b * NQ + q] for q in range(NQ)], 0) for b in range(B)], 0)
    return out.astype(np.float32)


def kernel(**inputs):
    inp = {k: np.asarray(v) for k, v in inputs.items()}
    return run_model(inp, 2, SEQ, NQ, 2)
```

```python
import math
from contextlib import ExitStack

import numpy as np
import concourse.bass as bass
import concourse.mybir as mybir
from concourse.bass_utils import run_bass_kernel_spmd

F32 = mybir.dt.float32
BF16 = mybir.dt.bfloat16
AF = mybir.ActivationFunctionType
ALU = mybir.AluOpType
AX = mybir.AxisListType

D = 1024
DFF = 4096
INW = 2816
EPS = 1e-6
NCORE = 8
NQ = 4
SEQ = 16384
NT_FULL = SEQ // NQ // 128
HALO = 16
RING = 18
PD = 2304
XJ = 2176
C_CONV, C_RQ, C_RK, C_RV, C_RG, C_AQ, C_AK, C_AV = 0, 512, 704, 896, 1280, 1664, 2048, 2432


class Prog:
    ENG = ('pe', 'act', 'dve', 'pool', 'sp')

    def __init__(self):
        self.ops = {e: [] for e in self.ENG}
        self.cnt = {e: 0 for e in self.ENG}
        self.dcnt = {}
        self.state = {}
        self.selfsync = {'pe': False, 'act': True, 'dve': True, 'pool': True, 'sp': False}

    def _deps(self, reads, writes):
        deps = {}

        def add(s, v):
            if deps.get(s, 0) < v:
                deps[s] = v
        for k in reads:
            st = self.state.get(k)
            if st and st[0]:
                add(*st[0])
        for k in writes:
            st = self.state.get(k)
            if st:
                if st[0]:
                    add(*st[0])
                for s, v in st[1].items():
                    add(s, v)
        return deps

    def _commit(self, sv, reads, writes):
        for k in reads:
            st = self.state.setdefault(k, [None, {}])
            if st[1].get(sv[0], 0) < sv[1]:
                st[1][sv[0]] = sv[1]
        for k in writes:
            self.state[k] = [sv, {}]

    def op(self, eng, method, reads=(), writes=(), **kw):
        fn = (lambda m, k: (lambda e: getattr(e, m)(**k)))(method, kw)
        deps = self._deps(reads, writes)
        self.cnt[eng] += 1
        sv = (eng, self.cnt[eng])
        if not self.selfsync[eng]:
            deps.pop(eng, None)
        self.ops[eng].append(('op', fn, deps, None))
        self._commit(sv, reads, writes)

    def dma(self, queue, sem, reads=(), writes=(), **kw):
        fn = (lambda k: (lambda e: e.dma_start(**k)))(kw)
        deps = self._deps(reads, writes)
        key = 'dma:' + sem
        self.dcnt[key] = self.dcnt.get(key, 0) + 16
        sv = (key, self.dcnt[key])
        self.ops[queue].append(('dma', fn, deps, key))
        self._commit(sv, reads, writes)

    def seal(self, sem, keys):
        key = 'dma:' + sem
        for k in keys:
            self.state[k] = [(key, self.dcnt[key]), {}]

    def final_wait_all(self, queue='sp'):
        deps = dict(self.dcnt)
        for e in ('pe', 'act', 'dve', 'pool'):
            if self.cnt[e]:
                deps[e] = self.cnt[e]
        self.ops[queue].append(('wait', None, deps, None))

    def emit(self, nc):
        with ExitStack() as es:
            semh = {}
            for e in ('pe', 'act', 'dve', 'pool'):
                semh[e] = es.enter_context(nc.semaphore('s_' + e))
            for k in self.dcnt:
                semh[k] = es.enter_context(nc.semaphore('s_' + k.replace(':', '_')))
            block = es.enter_context(nc.Block())
            names = {'pe': 'tensor', 'act': 'scalar', 'dve': 'vector', 'pool': 'gpsimd', 'sp': 'sync'}

            def mk(ename):
                ops = self.ops[ename]

                def body(eng):
                    waited = {}
                    for kind, fn, deps, key in ops:
                        for s, v in deps.items():
                            if waited.get(s, 0) >= v:
                                continue
                            eng.wait_ge(semh[s], v)
                            waited[s] = v
                        if kind == 'wait':
                            continue
                        ins = fn(eng)
                        if kind == 'op':
                            ins.then_inc(semh[ename], 1)
                        else:
                            ins.then_inc(semh[key], 16)
                return body
            for ename in self.ENG:
                if self.ops[ename]:
                    getattr(block, names[ename])(mk(ename))


class Ctx:
    def __init__(self):
        self.nc = bass.Bass("TRN2", target_bir_lowering=False)
        self.P = Prog()
        self.es = ExitStack()

    def sb(self, name, shape, dt):
        return self.es.enter_context(self.nc.sbuf_tensor(name, shape, dt))

    def ps(self, name, shape, dt):
        return self.es.enter_context(self.nc.psum_tensor(name, shape, dt))

    def din(self, name, shape, dt=F32):
        return self.nc.dram_tensor(name, list(shape), dt, kind="ExternalInput").ap()

    def dout(self, name, shape, dt=F32):
        return self.nc.dram_tensor(name, list(shape), dt, kind="ExternalOutput").ap()

    def finish(self):
        self.P.final_wait_all('sp')
        self.P.emit(self.nc)
        self.es.close()
        return self.nc


def bcast_rows(ap2d, n):
    return bass.AP(ap2d.tensor, ap2d.offset, [[0, 128], [1, n]])


def emit_norm_T(P, xt, xkey, Gb, xn, junk, ss, rstd, tp, hT_dst, hTkey, ident, tpkeys=('B0',)):
    P.op('act', 'activation', [xkey], ['junk', 'ss'], out=junk[:], in_=xt, func=AF.Square, accum_out=ss[:])
    P.op('act', 'activation', ['ss'], ['rstd'], out=rstd[:], in_=ss[:], func=AF.Ln, scale=1.0 / D, bias=EPS)
    P.op('act', 'activation', ['rstd'], ['rstd'], out=rstd[:], in_=rstd[:], func=AF.Exp, scale=-0.5)
    P.op('dve', 'scalar_tensor_tensor', [xkey, 'rstd', 'Gb'], ['xn'], out=xn[:], in0=xt, scalar=rstd[:, 0:1],
         in1=Gb[:], op0=ALU.mult, op1=ALU.mult)
    for kc in range(8):
        P.op('pe', 'transpose', ['xn', 'ident'], list(tpkeys), out=tp[:, kc * 128:(kc + 1) * 128],
             in_=xn[:, kc * 128:(kc + 1) * 128], identity=ident[:])
    P.op('act', 'activation', list(tpkeys), [hTkey], out=hT_dst, in_=tp[:].rearrange("p (k t) -> p k t", k=8), func=AF.Copy)


def emit_sigmoid(P, src, srckeys, dst, dstkey, tmp, tmpkey):
    P.op('act', 'activation', srckeys, [tmpkey], out=tmp, in_=src, func=AF.Exp, scale=-1.0)
    P.op('act', 'activation', [tmpkey], [tmpkey], out=tmp, in_=tmp, func=AF.Ln, bias=1.0)
    P.op('act', 'activation', [tmpkey], [dstkey], out=dst, in_=tmp, func=AF.Exp, scale=-1.0)


def build_stage_b(NT):
    C = Ctx()
    nc, P = C.nc, C.P
    T = NT * 128
    x1 = C.din("x1", [T, D])
    g2 = C.din("g2", [1, D])
    w1 = C.din("w1", [D, DFF])
    w2 = C.din("w2", [DFF, D])
    idn = C.din("idn", [128, 128])
    x2 = C.dout("x2", [T, D])
    W1 = C.sb("W1", [128, 8, DFF], BF16)
    W2 = C.sb("W2", [128, 32, D], BF16)
    Gb = C.sb("Gb", [128, D], F32)
    ident = C.sb("ident", [128, 128], BF16)
    NX = 4
    xs = [C.sb(f"xs{i}", [128, D], F32) for i in range(NX)]
    xn = C.sb("xn", [128, D], BF16)
    junk = C.sb("junk", [128, D], BF16)
    ss = C.sb("ss", [128, 1], F32)
    rstd = C.sb("rstd", [128, 1], F32)
    hT = [C.sb(f"hT{i}", [128, 8, 256], BF16) for i in range(2)]
    hid = C.sb("hid", [128, 32, 256], BF16)
    rl = [C.sb(f"rl{i}", [128, 512], F32) for i in range(2)]
    ys = [C.sb(f"ys{i}", [128, D], F32) for i in range(2)]
    tp = C.ps("tp", [128, D], BF16)
    ph = [C.ps(f"ph{i}", [128, 512], F32) for i in range(2)]
    po = [C.ps(f"po{i}", [128, 512], F32) for i in range(4)]

    P.dma('pool', 'ident', [], ['ident'], out=ident[:], in_=idn[:, :])
    P.dma('sp', 'Gb', [], ['Gb'], out=Gb[:], in_=bcast_rows(g2, D))
    w1v = w1.rearrange("(kc p) n -> p kc n", p=128)
    w2v = w2.rearrange("(kc p) n -> p kc n", p=128)
    for fb in range(8):
        P.dma('pool', f'W1_{fb}', [], [f'W1:{fb}'], out=W1[:, :, fb * 512:(fb + 1) * 512], in_=w1v[:, :, fb * 512:(fb + 1) * 512])
    for kb in range(8):
        P.dma('pool', f'W2_{kb}', [], [f'W2:{kb}'], out=W2[:, kb * 4:(kb + 1) * 4, :], in_=w2v[:, kb * 4:(kb + 1) * 4, :])

    def load_x(t):
        s = t % NX
        P.dma('sp', f'xs{s}', [], [f'xs{s}'], out=xs[s][:], in_=x1[t * 128:(t + 1) * 128, :])

    def prep(g):
        for i in range(2):
            t = 2 * g + i
            s = t % NX
            emit_norm_T(P, xs[s][:], f'xs{s}', Gb, xn, junk, ss, rstd, tp,
                        hT[g % 2][:, :, i * 128:(i + 1) * 128], f'hT{g % 2}:{i}', ident)

    NG = NT // 2
    for t in range(min(NX, NT)):
        load_x(t)
    prep(0)
    for g in range(NG):
        h = hT[g % 2]
        hk = [f'hT{g % 2}:0', f'hT{g % 2}:1']
        for fp in range(16):
            pb = ph[fp % 2]
            for half in range(2):
                fc = fp * 2 + half
                for kc in range(8):
                    P.op('pe', 'matmul', [f'W1:{fc // 4}'] + hk, [f'ph{fp % 2}'],
                         out=pb[:, half * 256:(half + 1) * 256], lhsT=W1[:, kc, fc * 128:(fc + 1) * 128],
                         rhs=h[:, kc, :], start=(kc == 0), stop=(kc == 7))
            r = rl[fp % 2]
            P.op('act', 'activation', [f'ph{fp % 2}'], [f'rl{fp % 2}'], out=r[:], in_=pb[:], func=AF.Relu)
            P.op('dve', 'tensor_tensor', [f'ph{fp % 2}', f'rl{fp % 2}'], [f'hid:{fp}'],
                 out=hid[:, 2 * fp:2 * fp + 2, :], in0=r[:].rearrange("p (a t) -> p a t", a=2),
                 in1=pb[:].rearrange("p (a t) -> p a t", a=2), op=ALU.mult)
        if g + 1 < NG:
            prep(g + 1)
        for i in range(2):
            t = 2 * g + i
            s = t % NX
            for dh in range(2):
                pi = i * 2 + dh
                pb = po[pi]
                for fc in range(32):
                    P.op('pe', 'matmul', [f'hid:{fc // 2}', f'W2:{fc // 4}'], [f'po{pi}'],
                         out=pb[:], lhsT=hid[:, fc, i * 128:(i + 1) * 128], rhs=W2[:, fc, dh * 512:(dh + 1) * 512],
                         start=(fc == 0), stop=(fc == 31))
                P.op('dve', 'tensor_tensor', [f'po{pi}', f'xs{s}'], [f'ys{i}'],
                     out=ys[i][:, dh * 512:(dh + 1) * 512], in0=pb[:], in1=xs[s][:, dh * 512:(dh + 1) * 512], op=ALU.add)
            P.dma('sp', f'ys{i}', [f'ys{i}'], [f'x2:{t}'], out=x2[t * 128:(t + 1) * 128, :], in_=ys[i][:])
            if t + NX < NT:
                load_x(t + NX)
    return C.finish()


def emit_rotary(P, pt, ptkey, cs, cskey, out, outkey, tmps, n):
    xe, xo = pt[:, 0:2 * n:2], pt[:, 1:2 * n:2]
    c, s = cs[:, 0, 0:n], cs[:, 1, 0:n]
    t1, t2 = tmps[0][:, 0:n], tmps[1][:, 0:n]
    P.op('dve', 'tensor_tensor', [ptkey, cskey], ['rt1'], out=t1, in0=xe, in1=c, op=ALU.mult)
    P.op('dve', 'tensor_tensor', [ptkey, cskey], ['rt2'], out=t2, in0=xo, in1=s, op=ALU.mult)
    P.op('dve', 'tensor_tensor', ['rt1', 'rt2'], [outkey], out=out[:, 0:2 * n:2], in0=t1, in1=t2, op=ALU.subtract)
    P.op('dve', 'tensor_tensor', [ptkey, cskey], ['rt1'], out=t1, in0=xo, in1=c, op=ALU.mult)
    P.op('dve', 'tensor_tensor', [ptkey, cskey], ['rt2'], out=t2, in0=xe, in1=s, op=ALU.mult)
    P.op('dve', 'tensor_tensor', ['rt1', 'rt2'], [outkey], out=out[:, 1:2 * n:2], in0=t1, in1=t2, op=ALU.add)


def build_stage_p(NT):
    C = Ctx()
    nc, P = C.nc, C.P
    T = NT * 128
    xin = C.din("xin", [T, D])
    g1 = C.din("g1", [1, D])
    win = C.din("win", [D, INW])
    idn = C.din("idn", [128, 128])
    csk = C.din("csk", [NT, 128, 2, 96])
    dend = C.din("dend", [NT, 128, 192])
    aout = C.dout("aout", [2, 96, 384])
    Wk = C.sb("Wk", [128, 8, 576], BF16)
    Gb = C.sb("Gb", [128, D], F32)
    ident = C.sb("ident", [128, 128], BF16)
    xs = [C.sb(f"xs{i}", [128, D], F32) for i in range(2)]
    xn = C.sb("xn", [128, D], BF16)
    junk = C.sb("junk", [128, D], BF16)
    ss = C.sb("ss", [128, 1], F32)
    rstd = C.sb("rstd", [128, 1], F32)
    hT = C.sb("hT", [128, 8, 128], BF16)
    cs = [C.sb(f"cs{i}", [128, 2, 96], F32) for i in range(2)]
    de = [C.sb(f"de{i}", [128, 192], F32) for i in range(2)]
    tmps = [C.sb(f"rtmp{i}", [128, 192], F32) for i in range(2)]
    krot = C.sb("krot", [128, 192], F32)
    kdec = C.sb("kdec", [128, 192], BF16)
    vb = C.sb("vb", [128, 384], BF16)
    ao = C.sb("ao", [96, 2, 384], F32)
    tp = C.ps("tp", [128, D], BF16)
    pk = C.ps("pk", [128, 512], F32)
    pv = C.ps("pv", [128, 512], F32)
    pa = [C.ps(f"pa{i}", [128, 512], F32) for i in range(2)]

    P.dma('pool', 'ident', [], ['ident'], out=ident[:], in_=idn[:, :])
    P.dma('sp', 'Gb', [], ['Gb'], out=Gb[:], in_=bcast_rows(g1, D))
    wv = win.rearrange("(kc p) n -> p kc n", p=128)
    for kc in range(8):
        P.dma('pool', 'Wk', [], ['Wk'], out=Wk[:, kc, 0:192], in_=wv[:, kc, C_RK:C_RK + 192])
        P.dma('pool', 'Wk', [], ['Wk'], out=Wk[:, kc, 192:576], in_=wv[:, kc, C_RV:C_RV + 384])
    P.seal('Wk', ['Wk'])

    def load(t):
        s = t % 2
        P.dma('sp', f'xs{s}', [], [f'xs{s}'], out=xs[s][:], in_=xin[t * 128:(t + 1) * 128, :])
        P.dma('sp', f'cs{s}', [], [f'cs{s}'], out=cs[s][:], in_=csk[t])
        P.dma('sp', f'de{s}', [], [f'de{s}'], out=de[s][:], in_=dend[t])

    load(0)
    for t in range(NT):
        s = t % 2
        if t + 1 < NT:
            load(t + 1)
        emit_norm_T(P, xs[s][:], f'xs{s}', Gb, xn, junk, ss, rstd, tp, hT[:], 'hT', ident)
        for kc in range(8):
            P.op('pe', 'matmul', ['Wk', 'hT'], ['pk'], out=pk[:, 0:192], lhsT=hT[:, kc, :], rhs=Wk[:, kc, 0:192],
                 start=(kc == 0), stop=(kc == 7))
        for kc in range(8):
            P.op('pe', 'matmul', ['Wk', 'hT'], ['pv'], out=pv[:, 0:384], lhsT=hT[:, kc, :], rhs=Wk[:, kc, 192:576],
                 start=(kc == 0), stop=(kc == 7))
        emit_rotary(P, pk, 'pk', cs[s], f'cs{s}', krot, 'krot', tmps, 96)
        P.op('dve', 'tensor_tensor', ['krot', f'de{s}'], ['kdec'], out=kdec[:], in0=krot[:], in1=de[s][:], op=ALU.mult)
        P.op('act', 'activation', ['pv'], ['vb'], out=vb[:], in_=pv[:, 0:384], func=AF.Copy)
        for blk in range(2):
            P.op('pe', 'matmul', ['kdec', 'vb'], [f'pa{blk}'], out=pa[blk][0:96, 0:384],
                 lhsT=kdec[:, blk * 96:(blk + 1) * 96], rhs=vb[:], start=(t == 0), stop=(t == NT - 1))
    for blk in range(2):
        P.op('act', 'activation', [f'pa{blk}'], ['ao'], out=ao[:, blk, :], in_=pa[blk][0:96, 0:384], func=AF.Copy)
    P.dma('sp', 'ao', ['ao'], ['aout'], out=aout.rearrange("b p n -> p b n"), in_=ao[:])
    return C.finish()


def build_stage_a(NT, dbg_tiles=None, dbg_phase=9):
    C = Ctx()
    nc, P = C.nc, C.P
    TT = (HALO + NT) * 128
    xin = C.din("xin", [TT, D])
    g1 = C.din("g1", [1, D])
    win = C.din("win", [D, INW])
    wout = C.din("wout", [D, D])
    cwT = C.din("cwT", [256, 31])
    cbg = C.din("cbg", [128, 4])
    retg = C.din("retg", [1, 384])
    qg = C.din("qg", [1, 64])
    kg = C.din("kg", [1, 64])
    relb = C.din("relb", [32, 6])
    idn = C.din("idn", [128, 128])
    bdg = C.din("bdg", [128, 128])
    csq = C.din("csq", [NT, 128, 2, 192])
    dect = C.din("dect", [128, 6, 128])
    qdt = C.din("qdt", [96, 2, 128])
    kdt = C.din("kdt", [128, 192])
    g128 = C.din("g128", [96, 2, 64])
    c2 = C.din("c2", [32, PD])
    flag = C.din("flag", [128, HALO])
    sin0 = C.din("sin0", [96, 8, 2, 64])
    scoef = C.din("scoef", [96, 8, 2, 64])
    rscr = nc.dram_tensor("rscr", [6, 128 * PD], BF16, kind="Internal").ap()
    x1 = C.dout("x1", [NT * 128, D])

    Win = C.sb("Win", [128, 8, INW], BF16)
    Wout = C.sb("Wout", [128, 8, D], BF16)
    Tab = C.sb("Tab", [128, 6, XJ], BF16)
    Dg = C.sb("Dg", [128, 2, 31, 128], BF16)
    KT = C.sb("KT", [128, 3, RING * 128], BF16)
    VA = C.sb("VA", [128, RING, 6, 65], BF16)
    Gb = C.sb("Gb", [128, D], F32)
    Rgb = C.sb("Rgb", [128, 384], F32)
    Qgb = C.sb("Qgb", [128, 384], F32)
    Kgb = C.sb("Kgb", [128, 384], F32)
    ident = C.sb("ident", [128, 128], BF16)
    identf = C.sb("identf", [128, 128], F32)
    Bd = C.sb("Bd", [128, 128], BF16)
    cw = C.sb("cw", [128, 2, 31], F32)
    cb = C.sb("cb", [128, 4], F32)
    decT = C.sb("decT", [128, 6, 128], F32)
    qdT = C.sb("qdT", [128, 2, 128], BF16)
    kdT = C.sb("kdT", [128, 192], F32)
    G128 = C.sb("G128", [96, 2, 64], F32)
    flg = C.sb("flg", [128, HALO], F32)
    xs = [C.sb(f"xs{i}", [128, D], F32) for i in range(2)]
    cs = [C.sb(f"cs{i}", [128, 2, 192], F32) for i in range(2)]
    xn = C.sb("xn", [128, D], BF16)
    junk = C.sb("junk", [128, D], BF16)
    ss = C.sb("ss", [128, 1], F32)
    rstd = C.sb("rstd", [128, 1], F32)
    hT = C.sb("hT", [128, 8, 128], BF16)
    hgl = C.sb("hgl", [128, 2, 160], BF16)
    sg = C.sb("sg", [128, 2, 128], F32)
    cf = C.sb("cf", [128, 2, 128], F32)
    sqb = C.sb("sqb", [128, 2, 128], BF16)
    rs = C.sb("rs", [128, 2, 128], F32)
    cn = C.sb("cn", [128, 2, 128], F32)
    mixT = C.sb("mixT", [128, 8, 128], BF16)
    tmps = [C.sb(f"rtmp{i}", [128, 192], F32) for i in range(2)]
    qkr = C.sb("qkr", [128, 384], F32)
    qkb = C.sb("qkb", [128, 384], BF16)
    kdec = C.sb("kdec", [128, 192], BF16)
    qT = C.sb("qT", [128, 2, 128], BF16)
    kTr = C.sb("kTr", [128, 2, 128], BF16)
    qdecT = C.sb("qdecT", [128, 2, 128], BF16)
    rvb = C.sb("rvb", [128, 384], BF16)
    PTr = C.sb("PTr", [128, 6, 128], BF16)
    Sst = C.sb("Sst", [96, 2, 64], F32)
    Sbf = C.sb("Sbf", [96, 2, 64], BF16)
    f384 = [C.sb(f"f384_{i}", [128, 384], F32) for i in range(3)]
    QT = C.sb("QT", [128, 3, 128], BF16)
    eb = [C.sb(f"eb{i}", [128, 512], BF16) for i in range(2)]
    PT = [C.sb(f"PT{i}", [128, 512], BF16) for i in range(2)]
    ys = C.sb("ys", [128, D], F32)
    Esb = C.sb("Esb", [32, 6], F32)
    B = [C.ps(f"B{i}", [128, 512], F32) for i in range(1, 8)]
    B0 = C.ps("B0", [128, D], BF16)
    tp = B0
    B1, B2, B3, B4, B5, B6, B7 = B

    P.dma('pool', 'c_ident', [], ['ident'], out=ident[:], in_=idn[:, :])
    P.dma('sp', 'c_identf', [], ['identf'], out=identf[:], in_=idn[:, :])
    P.dma('pool', 'c_bd', [], ['Bd'], out=Bd[:], in_=bdg[:, :])
    P.dma('sp', 'c_Gb', [], ['Gb'], out=Gb[:], in_=bcast_rows(g1, D))
    P.dma('sp', 'c_Rgb', [], ['Rgb'], out=Rgb[:], in_=bcast_rows(retg, 384))
    P.dma('sp', 'c_Qgb', [], ['Qgb'], out=Qgb[:].rearrange("p (h d) -> p h d", h=6),
          in_=bass.AP(qg.tensor, 0, [[0, 128], [0, 6], [1, 64]]))
    P.dma('sp', 'c_Kgb', [], ['Kgb'], out=Kgb[:].rearrange("p (h d) -> p h d", h=6),
          in_=bass.AP(kg.tensor, 0, [[0, 128], [0, 6], [1, 64]]))
    P.dma('sp', 'c_cw', [], ['cw'], out=cw[:], in_=cwT.rearrange("(c p) j -> p c j", p=128))
    P.dma('sp', 'c_cb', [], ['cb'], out=cb[:], in_=cbg[:, :])
    P.dma('sp', 'c_decT', [], ['decT'], out=decT[:], in_=dect[:, :, :])
    P.op('pool', 'memset', [], ['qdT'], ap=qdT[:].rearrange("p b t -> p (b t)"), constant=0.0)
    P.dma('pool', 'c_qdT', [], ['qdT'], out=qdT[0:96, :, :], in_=qdt[:, :, :])
    P.dma('sp', 'c_kdT', [], ['kdT'], out=kdT[:], in_=kdt[:, :])
    P.dma('sp', 'c_g128', [], ['G128'], out=G128[:], in_=g128[:, :, :])
    P.dma('sp', 'c_flag', [], ['flg'], out=flg[:], in_=flag[:, :])
    P.dma('sp', 'c_E', [], ['Esb'], out=Esb[:], in_=relb[:, :])
    wv = win.rearrange("(kc p) n -> p kc n", p=128)
    wov = wout.rearrange("(kc p) n -> p kc n", p=128)
    for kc in range(8):
        P.dma('pool', 'Win', [], ['Win'], out=Win[:, kc, :], in_=wv[:, kc, :])
    for kc in range(8):
        P.dma('pool', 'Wout', [], ['Wout'], out=Wout[:, kc, :], in_=wov[:, kc, :])
    P.seal('Win', ['Win'])
    P.seal('Wout', ['Wout'])

    P.op('act', 'activation', ['Esb'], ['Esb'], out=Esb[:], in_=Esb[:], func=AF.Exp)
    nch = (PD + 511) // 512
    ktkeys = [f'KT{i}' for i in range(RING)]
    Fsb = KT[0:6, 0, :]
    for ci in range(nch):
        lo = ci * 512
        n = min(512, PD - lo)
        P.dma('sp', 'c_C2', [], ['ys'], out=ys[0:32, 0:n], in_=c2[:, lo:lo + n])
        P.op('pe', 'matmul', ['Esb', 'ys'], ['B1'], out=B1[0:6, 0:n], lhsT=Esb[:], rhs=ys[0:32, 0:n],
             start=True, stop=True)
        P.op('act', 'activation', ['B1'], ktkeys, out=KT[0:6, 0, lo:lo + n], in_=B1[0:6, 0:n], func=AF.Copy)
    P.dma('sp', 'c_rscr', ktkeys, ['rscr'], out=rscr.rearrange("h (r n) -> h r n", r=128),
          in_=bass.AP(Fsb.tensor, Fsb.offset, [[3 * RING * 128, 6], [0, 128], [1, PD]]))
    for h in range(6):
        P.dma('sp', 'c_Tab', ['rscr'], ['Tab'], out=Tab[:, h, :],
              in_=bass.AP(rscr.tensor, h * 128 * PD + 127, [[PD - 1, 128], [1, XJ]]))
    P.seal('c_Tab', ['Tab'])
    for c in range(2):
        for j in range(31):
            P.op('pool', 'tensor_scalar', ['identf', 'cw'], ['Dg'], out=Dg[:, c, j, :], in0=identf[:],
                 scalar1=cw[:, c, j:j + 1], scalar2=None, op0=ALU.mult)
    P.op('pool', 'memset', [], ['VA'], ap=VA[:].rearrange("p r h e -> p (r h e)"), constant=1.0)
    big = xs[0][0:96, :].rearrange("p (r x) -> p r x", r=8)
    big2 = xs[1][0:96, :].rearrange("p (r x) -> p r x", r=8)
    P.dma('sp', 'c_s0', [], ['xs0'], out=big, in_=sin0.rearrange("p r b e -> p r (b e)"))
    P.dma('sp', 'c_s1', [], ['xs1'], out=big2, in_=scoef.rearrange("p r b e -> p r (b e)"))
    P.op('dve', 'tensor_tensor', ['xs0', 'xs1'], ['xs0'], out=big, in0=big, in1=big2, op=ALU.mult)
    P.op('dve', 'tensor_reduce', ['xs0'], ['Sst'], out=Sst[:].rearrange("p b e -> p (b e)"),
         in_=big.rearrange("p r x -> p x r"), axis=AX.X, op=ALU.add)
    P.op('dve', 'tensor_copy', ['Sst'], ['Sbf'], out=Sbf[:], in_=Sst[:])

    def load(ti):
        s = ti % 2
        P.dma('sp', f'xs{s}', [], [f'xs{s}'], out=xs[s][:], in_=xin[ti * 128:(ti + 1) * 128, :])
        t = ti - HALO
        if t >= 0:
            P.dma('sp', f'cs{s}', [], [f'cs{s}'], out=cs[s][:], in_=csq[t])

    def proj_tok(col, n, bank, bkey):
        for kc in range(8):
            P.op('pe', 'matmul', ['Win', 'hT'], [bkey], out=bank[:, 0:n], lhsT=hT[:, kc, :],
                 rhs=Win[:, kc, col:col + n], start=(kc == 0), stop=(kc == 7))

    def head_rstd(src, srckeys, n_over):
        P.op('act', 'activation', srckeys, ['sq384'], out=f384[3][:], in_=src, func=AF.Square)
        P.op('dve', 'tensor_reduce', ['sq384'], ['s6'], out=s6[:], in_=f384[3][:].rearrange("p (h d) -> p h d", h=6),
             axis=AX.X, op=ALU.add)
        P.op('act', 'activation', ['s6'], ['r6'], out=r6[:], in_=s6[:], func=AF.Ln, scale=1.0 / n_over, bias=EPS)
        P.op('act', 'activation', ['r6'], ['r6'], out=r6[:], in_=r6[:], func=AF.Exp, scale=-0.5)

    def r6b():
        return r6[:].unsqueeze(2).to_broadcast([128, 6, 64])

    def v3(ap):
        return ap.rearrange("p (h d) -> p h d", h=6)

    sqj = {k: C.sb(f"sqj_{k}", [128, 384], BF16) for k in ('r', 'k', 'q')}
    s6c = {k: C.sb(f"s6_{k}", [128, 6], F32) for k in ('r', 'k', 'q')}
    r6c = {k: C.sb(f"r6_{k}", [128, 6], F32) for k in ('r', 'k', 'q', 'a')}
    bo = {k: C.sb(f"bo_{k}", [128, 384], BF16) for k in ('r', 'k', 'q', 'a')}
    BK = {'B1': B1, 'B2': B2, 'B3': B3, 'B4': B4, 'B5': B5, 'B6': B6, 'B7': B7}
    B0K = ['B0a', 'B0b', 'B0c']

    def proj(col, n, bk):
        bank = BK[bk]
        for kc in range(8):
            P.op('pe', 'matmul', ['Win', 'hT'], [bk], out=bank[:, 0:n], lhsT=hT[:, kc, :],
                 rhs=Win[:, kc, col:col + n], start=(kc == 0), stop=(kc == 7))

    def rstd6(tag, src, srckeys):
        P.op('act', 'activation', srckeys, ['sqj' + tag], out=sqj[tag][:], in_=src, func=AF.Square)
        P.op('dve', 'tensor_reduce', ['sqj' + tag], ['s6' + tag], out=s6c[tag][:],
             in_=sqj[tag][:].rearrange("p (h d) -> p h d", h=6), axis=AX.X, op=ALU.add)
        P.op('act', 'activation', ['s6' + tag], ['r6' + tag], out=r6c[tag][:], in_=s6c[tag][:], func=AF.Ln,
             scale=1.0 / 64, bias=EPS)
        P.op('act', 'activation', ['r6' + tag], ['r6' + tag], out=r6c[tag][:], in_=r6c[tag][:], func=AF.Exp, scale=-0.5)

    def rb(tag):
        return r6c[tag][:].unsqueeze(2).to_broadcast([128, 6, 64])

    def tr3(src, srckey, c0, dst, dstkey):
        keys = sorted({B0K[(c0 + i * 128) // 384] for i in range(3)} | {B0K[(c0 + i * 128 + 127) // 384] for i in range(3)})
        for i in range(3):
            P.op('pe', 'transpose', [srckey, 'ident'], keys, out=tp[:, c0 + i * 128:c0 + (i + 1) * 128],
                 in_=src[:, i * 128:(i + 1) * 128], identity=ident[:])
        P.op('act', 'activation', keys, [dstkey], out=dst,
             in_=tp[:, c0:c0 + 384].rearrange("p (c t) -> p c t", c=3), func=AF.Copy)

    load(0)
    for ti in range(HALO + NT):
        t = ti - HALO
        s = ti % 2
        slot = ti % RING
        if ti + 1 < HALO + NT:
            load(ti + 1)
        own = t >= 0
        if dbg_tiles is not None and ti >= dbg_tiles:
            continue
        emit_norm_T(P, xs[s][:], f'xs{s}', Gb, xn, junk, ss, rstd, tp, hT[:], 'hT', ident, tpkeys=B0K)

        if own or t == -1:
            for cc in range(4):
                for kc in range(8):
                    P.op('pe', 'matmul', ['Win', 'hT'], ['B1'], out=B1[:, cc * 128:(cc + 1) * 128],
                         lhsT=Win[:, kc, cc * 128:(cc + 1) * 128], rhs=hT[:, kc, :], start=(kc == 0), stop=(kc == 7))
        if own:
            proj(C_RQ, 384, 'B3')
            proj(C_RV, 384, 'B4')
        proj(C_AK, 384, 'B7')
        if not own:
            proj(C_AV, 384, 'B6')

        if own or t == -1:
            if own:
                P.op('pool', 'tensor_copy', ['hgl'], ['hgl'], out=hgl[:, :, 0:30], in_=hgl[:, :, 128:158])
            emit_sigmoid(P, B1[:, 256:512].rearrange("p (c t) -> p c t", c=2), ['B1'], sg[:], 'sg', sg[:], 'sg')
            if own:
                P.op('dve', 'tensor_tensor', ['B1', 'sg'], ['hgl'], out=hgl[:, :, 30:158],
                     in0=B1[:, 0:256].rearrange("p (c t) -> p c t", c=2), in1=sg[:], op=ALU.mult)
            else:
                P.op('dve', 'scalar_tensor_tensor', ['B1', 'sg', 'flg'], ['hgl'], out=hgl[:, :, 30:158],
                     in0=B1[:, 0:256].rearrange("p (c t) -> p c t", c=2), scalar=flg[:, ti:ti + 1], in1=sg[:],
                     op0=ALU.mult, op1=ALU.mult)

        rstd6('k', B7[:, 0:384], ['B7'])
        P.op('dve', 'tensor_tensor', ['B7', 'r6k'], ['f2'], out=v3(f384[2][:]), in0=v3(B7[:, 0:384]), in1=rb('k'), op=ALU.mult)
        P.op('dve', 'tensor_tensor', ['f2', 'Kgb'], ['bok'], out=bo['k'][:], in0=f384[2][:], in1=Kgb[:], op=ALU.mult)

        if not own:
            tr3(bo['k'], 'bok', 0, KT[:, :, slot * 128:(slot + 1) * 128], f'KT{slot}')
            P.op('act', 'activation', ['B6', 'VA', 'flg'], [f'VA{slot}'], out=VA[:, slot, :, 0:64], in_=v3(B6[:, 0:384]),
                 func=AF.Copy, scale=flg[:, ti:ti + 1])
            P.op('dve', 'tensor_copy', ['flg', 'VA'], [f'VA{slot}'], out=VA[:, slot, :, 64:65],
                 in_=flg[:, ti:ti + 1].unsqueeze(1).to_broadcast([128, 6, 1]))
            continue

        emit_rotary(P, B3, 'B3', cs[s], f'cs{s}', qkr, 'qkr', tmps, 192)
        P.op('act', 'activation', ['qkr'], ['qkb'], out=qkb[:], in_=qkr[:], func=AF.Copy)
        P.op('dve', 'tensor_tensor', ['qkr', 'kdT'], ['kdec'], out=kdec[:], in0=qkr[:, 192:384], in1=kdT[:], op=ALU.mult)
        P.op('act', 'activation', ['B4'], ['rvb'], out=rvb[:], in_=B4[:, 0:384], func=AF.Copy)
        for c in range(2):
            for j in range(31):
                P.op('pe', 'matmul', ['Dg', 'hgl'], ['B2'], out=B2[:, c * 128:(c + 1) * 128],
                     lhsT=Dg[:, c, j, :], rhs=hgl[:, c, j:j + 128], start=(j == 0), stop=(j == 30))
        proj(C_RG, 384, 'B1')
        proj(C_AV, 384, 'B6')
        for i in range(4):
            P.op('pe', 'transpose', ['qkb', 'ident'], ['B0a', 'B0b'], out=tp[0:96, i * 128:(i + 1) * 128],
                 in_=qkb[:, i * 96:(i + 1) * 96], identity=ident[:])
        P.op('act', 'activation', ['B0a', 'B0b'], ['qT'], out=qT[:], in_=tp[:, 0:256].rearrange("p (b t) -> p b t", b=2), func=AF.Copy)
        P.op('act', 'activation', ['B0a', 'B0b'], ['kTr'], out=kTr[:], in_=tp[:, 256:512].rearrange("p (b t) -> p b t", b=2), func=AF.Copy)
        P.op('dve', 'tensor_tensor', ['qT', 'qdT'], ['qdecT'], out=qdecT[:], in0=qT[:], in1=qdT[:], op=ALU.mult)
        for c in range(2):
            P.op('act', 'activation', ['B2', 'cb'], ['cf'], out=cf[:, c, :], in_=B2[:, c * 128:(c + 1) * 128],
                 func=AF.Identity, bias=cb[:, c:c + 1])
            P.op('act', 'activation', ['B2', 'cb'], ['sqb'], out=sqb[:, c, :], in_=B2[:, c * 128:(c + 1) * 128],
                 func=AF.Square, bias=cb[:, c:c + 1])
        P.op('act', 'activation', ['B6', 'VA'], [f'VA{slot}'], out=VA[:, slot, :, 0:64], in_=v3(B6[:, 0:384]), func=AF.Copy)
        if ti >= RING:
            P.op('pool', 'memset', ['VA'], [f'VA{slot}'], ap=VA[:, slot, :, 64:65], constant=1.0)
        sbanks = [(B5, 'B5'), (B4, 'B4'), (B3, 'B3')]
        for h in range(6):
            blk, a = h // 3, h % 3
            bank, bk = sbanks[a]
            P.op('pe', 'matmul', ['kTr', 'qT'], [bk], out=bank[:, blk * 128:(blk + 1) * 128],
                 lhsT=kTr[32 * a:32 * a + 32, blk, :], rhs=qT[32 * a:32 * a + 32, blk, :], start=True, stop=True)
        for c in range(2):
            P.op('pe', 'matmul', ['Bd', 'sqb'], ['B2'], out=B2[:, 256 + c * 128:256 + (c + 1) * 128],
                 lhsT=Bd[:], rhs=sqb[:, c, :], start=True, stop=True)
        proj(C_AQ, 384, 'B6')
        for a in range(3):
            bank, bk = sbanks[a]
            P.op('dve', 'tensor_tensor', [bk, 'decT'], ['PTr'], out=PTr[:, a::3, :],
                 in0=bank[:, 0:256].rearrange("p (h t) -> p h t", h=2), in1=decT[:, a::3, :], op=ALU.mult)
        P.op('act', 'activation', ['B2'], ['rs'], out=rs[:], in_=B2[:, 256:512].rearrange("p (c t) -> p c t", c=2),
             func=AF.Ln, bias=EPS)
        P.op('act', 'activation', ['rs'], ['rs'], out=rs[:], in_=rs[:], func=AF.Exp, scale=-0.5)
        for h in range(6):
            blk, a = h // 3, h % 3
            P.op('pe', 'matmul', ['PTr', 'rvb'], ['B5'], out=B5[:, h * 64:(h + 1) * 64], lhsT=PTr[:, h, :],
                 rhs=rvb[:, h * 64:(h + 1) * 64], start=True, stop=False)
            P.op('pe', 'matmul', ['qdecT', 'Sbf'], ['B5'], out=B5[:, h * 64:(h + 1) * 64],
                 lhsT=qdecT[32 * a:32 * a + 32, blk, :], rhs=Sbf[32 * a:32 * a + 32, blk, :], start=False, stop=True)
        P.op('pe', 'matmul', ['kdec', 'rvb'], ['B4'], out=B4[0:96, 0:384], lhsT=kdec[:, 0:96], rhs=rvb[:], start=True, stop=True)
        P.op('pe', 'matmul', ['kdec', 'rvb'], ['B3'], out=B3[0:96, 0:384], lhsT=kdec[:, 96:192], rhs=rvb[:], start=True, stop=True)
        tr3(bo['k'], 'bok', 512, KT[:, :, slot * 128:(slot + 1) * 128], f'KT{slot}')
        for c in range(2):
            P.op('dve', 'scalar_tensor_tensor', ['cf', 'rs', 'cb'], ['cn'], out=cn[:, c, :], in0=cf[:, c, :],
                 scalar=cb[:, 2 + c:3 + c], in1=rs[:, c, :], op0=ALU.mult, op1=ALU.mult)
        emit_sigmoid(P, cn[:], ['cn'], sg[:], 'sg', sg[:], 'sg')
        P.op('dve', 'tensor_tensor', ['cn', 'sg'], ['mixT:c'], out=mixT[:, 0:2, :], in0=cn[:], in1=sg[:], op=ALU.mult)
        rstd6('q', B6[:, 0:384], ['B6'])
        P.op('dve', 'tensor_tensor', ['B6', 'r6q'], ['f2'], out=v3(f384[2][:]), in0=v3(B6[:, 0:384]), in1=rb('q'), op=ALU.mult)
        P.op('dve', 'tensor_tensor', ['f2', 'Qgb'], ['boq'], out=bo['q'][:], in0=f384[2][:], in1=Qgb[:], op=ALU.mult)
        tr3(bo['q'], 'boq', 0, QT[:], 'QT')
        for h in range(6):
            blk, a = h // 3, h % 3
            bank, bk = (B4, 'B4') if blk == 0 else (B3, 'B3')
            P.op('dve', 'tensor_tensor', ['Sst', 'G128'], ['Sst'], out=Sst[32 * a:32 * a + 32, blk, :],
                 in0=Sst[32 * a:32 * a + 32, blk, :], in1=G128[32 * a:32 * a + 32, blk, :], op=ALU.mult)
            P.op('dve', 'tensor_tensor', ['Sst', bk], ['Sst'], out=Sst[32 * a:32 * a + 32, blk, :],
                 in0=Sst[32 * a:32 * a + 32, blk, :], in1=bank[32 * a:32 * a + 32, h * 64:(h + 1) * 64], op=ALU.add)
        rstd6('r', B5[:, 0:384], ['B5'])
        P.op('dve', 'tensor_tensor', ['B5', 'r6r'], ['f0'], out=v3(f384[0][:]), in0=v3(B5[:, 0:384]), in1=rb('r'), op=ALU.mult)
        emit_sigmoid(P, B1[:, 0:384], ['B1'], f384[1][:], 'f1', f384[1][:], 'f1')
        P.op('dve', 'tensor_tensor', ['B1', 'f1'], ['f1'], out=f384[1][:], in0=B1[:, 0:384], in1=f384[1][:], op=ALU.mult)
        P.op('dve', 'tensor_tensor', ['f1', 'Rgb'], ['f1'], out=f384[1][:], in0=f384[1][:], in1=Rgb[:], op=ALU.mult)
        P.op('dve', 'tensor_tensor', ['f0', 'f1'], ['bor'], out=bo['r'][:], in0=f384[0][:], in1=f384[1][:], op=ALU.mult)
        P.op('dve', 'tensor_copy', ['Sst'], ['Sbf'], out=Sbf[:], in_=Sst[:])

        kts = list(range(ti, ti - 17, -1))
        chunks = [kts[i:i + 4] for i in range(0, 17, 4)]
        seq = [(h, ch) for h in range(6) for ch in chunks]
        sb2 = [(B3, 'B3'), (B4, 'B4')]

        def emit_st(i):
            h, ch = seq[i]
            pr, half = h // 2, h % 2
            bank, bk = sb2[i % 2]
            for bi, kt in enumerate(ch):
                ks = kt % RING
                P.op('pe', 'matmul', [f'KT{ks}', 'QT'], [bk], out=bank[:, bi * 128:(bi + 1) * 128],
                     lhsT=KT[64 * half:64 * half + 64, pr, ks * 128:(ks + 1) * 128],
                     rhs=QT[64 * half:64 * half + 64, pr, :], start=True, stop=True)

        emit_st(0)
        ret_T_done = False
        for i, (h, ch) in enumerate(seq):
            bank, bk = sb2[i % 2]
            e_, ek = eb[i % 2], f'eb{i % 2}'
            p_, pk_ = PT[i % 2], f'PT{i % 2}'
            n = len(ch) * 128
            if i + 1 < len(seq):
                emit_st(i + 1)
            if not ret_T_done and i == 1:
                tr3(bo['r'], 'bor', 384, mixT[:, 2:5, :], 'mixT:r')
                ret_T_done = True
            P.op('act', 'activation', [bk], [ek], out=e_[:, 0:n], in_=bank[:, 0:n], func=AF.Exp, scale=0.125)
            off = 128 * (ti - ch[0])
            P.op('dve', 'tensor_tensor', [ek, 'Tab'], [pk_], out=p_[:, 0:n], in0=e_[:, 0:n],
                 in1=Tab[:, h, off:off + n], op=ALU.mult)
            for bi, kt in enumerate(ch):
                ks = kt % RING
                P.op('pe', 'matmul', [pk_, f'VA{ks}'], ['B7'], out=B7[:, h * 65:(h + 1) * 65],
                     lhsT=p_[:, bi * 128:(bi + 1) * 128], rhs=VA[:, ks, h, :],
                     start=(kt == kts[0]), stop=(kt == kts[-1]))
        b7v = B7[:, 0:390].rearrange("p (h e) -> p h e", h=6)
        P.op('dve', 'reciprocal', ['B7'], ['r6a'], out=r6c['a'][:].unsqueeze(2), in_=b7v[:, :, 64:65])
        P.op('dve', 'tensor_tensor', ['B7', 'r6a'], ['boa'], out=v3(bo['a'][:]), in0=b7v[:, :, 0:64], in1=rb('a'), op=ALU.mult)
        tr3(bo['a'], 'boa', 0, mixT[:, 5:8, :], 'mixT:a')

        for dh in range(2):
            bank, bk = (B1, 'B1') if dh == 0 else (B2, 'B2')
            for c in range(8):
                P.op('pe', 'matmul', ['mixT:c', 'mixT:r', 'mixT:a', 'Wout'], [bk], out=bank[:],
                     lhsT=mixT[:, c, :], rhs=Wout[:, c, dh * 512:(dh + 1) * 512], start=(c == 0), stop=(c == 7))
            P.op('dve', 'tensor_tensor', [bk, f'xs{s}'], ['ys'], out=ys[:, dh * 512:(dh + 1) * 512], in0=bank[:],
                 in1=xs[s][:, dh * 512:(dh + 1) * 512], op=ALU.add)
        P.dma('sp', 'ys', ['ys'], [f'x1:{t}'], out=x1[t * 128:(t + 1) * 128, :], in_=ys[:])
    return C.finish()


_GAM = 1.0 - 2.0 ** (-5.0 - np.arange(6, dtype=np.float64))


def _t5_bucket(dist):
    nf = np.maximum(dist, 1).astype(np.float32)
    large = 16 + (np.log(nf / np.float32(16)) / np.float32(math.log(2048 / 16)) * np.float32(16)).astype(np.int32)
    large = np.minimum(large, 31)
    return np.where(dist < 16, dist, large)


def _const_tables():
    idn = np.eye(128, dtype=np.float32)
    bdg = np.zeros((128, 128), np.float32)
    bdg[:64, :64] = 1.0 / 64
    bdg[64:, 64:] = 1.0 / 64
    j = np.arange(128)
    diff = j[None, :] - j[:, None]
    dect = np.zeros((128, 6, 128), np.float64)
    for h in range(6):
        dect[:, h, :] = np.where(diff >= 0, _GAM[h] ** np.maximum(diff, 0), 0.0)
    qdt = np.zeros((96, 2, 128), np.float64)
    g128 = np.zeros((96, 2, 64), np.float64)
    for blk in range(2):
        for a in range(3):
            h = 3 * blk + a
            qdt[32 * a:32 * a + 32, blk, :] = (_GAM[h] ** (j + 1.0))[None, :]
            g128[32 * a:32 * a + 32, blk, :] = _GAM[h] ** 128.0
    kdt = np.zeros((128, 192), np.float64)
    for h in range(6):
        kdt[:, 32 * h:32 * h + 32] = (_GAM[h] ** (127.0 - j))[:, None]
    dd = np.arange(PD) - 127
    valid = (dd >= 0) & (dd <= 2048)
    dc = np.clip(dd, 0, 2048)
    mult = ((dc <= 128).astype(np.float64) + ((dc <= 512) & (dc % 4 == 0)) + ((dc <= 2048) & (dc % 16 == 0)))
    bucket = _t5_bucket(dc.astype(np.int32))
    c2 = np.zeros((32, PD), np.float64)
    c2[bucket, np.arange(PD)] = np.where(valid, mult, 0.0)
    f = lambda a: np.ascontiguousarray(a, dtype=np.float32)
    return dict(idn=idn, bdg=bdg, dect=f(dect), qdt=f(qdt), kdt=f(kdt), g128=f(g128), c2=f(c2))


def _rot_tables(pos0, NT):
    T = NT * 128
    pos = (pos0 + np.arange(T)).astype(np.float32)
    inv = (1.0 / (np.float32(10000.0) ** np.linspace(0.0, 1.0, 16, dtype=np.float32))).astype(np.float32)
    ang = (pos[:, None] * inv[None, :]).astype(np.float32).astype(np.float64)
    c, s = np.cos(ang), np.sin(ang)
    sc = 32.0 ** -0.5
    cq = np.tile(c, (1, 6))
    sq = np.tile(s, (1, 6))
    csq = np.stack([np.concatenate([cq, cq * sc], 1), np.concatenate([sq, sq * sc], 1)], 1)
    csk = np.stack([cq * sc, sq * sc], 1)
    loc = np.arange(T, dtype=np.float64)
    dend = np.zeros((T, 192), np.float64)
    for h in range(6):
        dend[:, 32 * h:32 * h + 32] = (_GAM[h] ** (T - 1.0 - loc))[:, None]
    f = lambda a, shp: np.ascontiguousarray(a.reshape(shp), dtype=np.float32)
    return f(csq, (NT, 128, 2, 192)), f(csk, (NT, 128, 2, 96)), f(dend, (NT, 128, 192))


_PROGS = {}


def _prog(kind, NT):
    key = (kind, NT)
    if key not in _PROGS:
        _PROGS[key] = {'p': build_stage_p, 'a': build_stage_a, 'b': build_stage_b}[kind](NT)
    return _PROGS[key]


def _run(nc, in_maps):
    res = run_bass_kernel_spmd(nc, in_maps, core_ids=list(range(len(in_maps))))
    return res.results


def run_model(inp, B, S, NQ, depth):
    NT = S // NQ // 128
    T = NT * 128
    ncore = B * NQ
    ct = _const_tables()
    x = np.ascontiguousarray(inp["x"], dtype=np.float32)
    rot = [_rot_tables((c % NQ) * T, NT) for c in range(ncore)]
    cur = [x[c // NQ, (c % NQ) * T:(c % NQ + 1) * T] for c in range(ncore)]
    f32 = lambda a: np.ascontiguousarray(a, dtype=np.float32)
    for l in range(depth):
        g1 = f32(inp["norm1_g"][l][None])
        win = f32(inp["w_in"][l])
        maps = [dict(xin=f32(cur[c]), g1=g1, win=win, idn=ct["idn"], csk=rot[c][1], dend=rot[c][2])
                for c in range(ncore)]
        outs = _run(_prog('p', NT), maps)
        sin0 = np.zeros((96, ncore, 2, 64), np.float32)
        for r in range(ncore):
            a_r = outs[r]["aout"]
            for blk in range(2):
                for a in range(3):
                    h = 3 * blk + a
                    sin0[32 * a:32 * a + 32, r, blk, :] = a_r[blk, 32 * a:32 * a + 32, 64 * h:64 * h + 64]
        if ncore < 8:
            sin0 = np.concatenate([sin0, np.zeros((96, 8 - ncore, 2, 64), np.float32)], 1)
        full = [np.concatenate([cur[b * NQ + q] for q in range(NQ)], 0) for b in range(B)]
        maps = []
        for c in range(ncore):
            b, q = c // NQ, c % NQ
            start = q * T
            halo = np.zeros((HALO * 128, D), np.float32)
            lo = max(0, start - HALO * 128)
            if start > lo:
                halo[HALO * 128 - (start - lo):] = full[b][lo:start]
            flag = np.zeros((128, HALO), np.float32)
            for i in range(HALO):
                if start - (HALO - i) * 128 >= 0:
                    flag[:, i] = 1.0
            scoef = np.zeros((96, 8, 2, 64), np.float64)
            for r in range(ncore):
                if r // NQ == b and r % NQ < q:
                    for blk in range(2):
                        for a in range(3):
                            scoef[32 * a:32 * a + 32, r, blk, :] = _GAM[3 * blk + a] ** (float(T) * (q - r % NQ - 1))
            cbg = np.concatenate([inp["conv_b"][l].reshape(2, 128).T, inp["conv_g"][l].reshape(2, 128).T], 1)
            maps.append(dict(
                xin=f32(np.concatenate([halo, cur[c]], 0)), g1=g1, win=win, wout=f32(inp["w_out"][l]),
                cwT=f32(inp["conv_w"][l].T), cbg=f32(cbg), retg=f32(inp["ret_g"][l][None]),
                qg=f32(inp["q_g"][l][None]), kg=f32(inp["k_g"][l][None]), relb=f32(inp["rel_bias"]),
                idn=ct["idn"], bdg=ct["bdg"], csq=rot[c][0], dect=ct["dect"], qdt=ct["qdt"], kdt=ct["kdt"],
                g128=ct["g128"], c2=ct["c2"], flag=flag, sin0=sin0, scoef=f32(scoef)))
        outs = _run(_prog('a', NT), maps)
        cur = [outs[c]["x1"] for c in range(ncore)]
        g2 = f32(inp["norm2_g"][l][None])
        w1 = f32(inp["w_ff1"][l])
        w2 = f32(inp["w_ff2"][l])
        maps = [dict(x1=f32(cur[c]), g2=g2, w1=w1, w2=w2, idn=ct["idn"]) for c in range(ncore)]
        outs = _run(_prog('b', NT), maps)
        cur = [outs[c]["x2"] for c in range(ncore)]
    out = np.stack([np.concatenate([cur[b * NQ + q] for q in range(NQ)], 0) for b in range(B)], 0)
    return out.astype(np.float32)


def kernel(**inputs):
    inp = {k: np.asarray(v) for k, v in inputs.items()}
    return run_model(inp, 2, SEQ, NQ, 2)
```
